# Optimizing a Trainium2 kernel written in Bass

```python
import math
import jax, jax.numpy as jnp
from jax import lax
import numpy as np

D_MODEL = 2048
BATCH = 4
SEQ = 2048
DEPTH = 1
DEC_BATCH = 128
DEC_SEQ = 8
PAST_LEN = 16384
PAGE_SIZE = 128

D_MLSTM = D_MODEL // 2
N_HEADS = 4
HEAD_DIM = D_MLSTM // N_HEADS
D_POOL = D_MODEL - D_MLSTM
POOL_WINDOWS = (2, 4, 8, 16)
N_POOL_GROUPS = len(POOL_WINDOWS)
POOL_GROUP = D_POOL // N_POOL_GROUPS
POOL_BUF = max(POOL_WINDOWS) - 1
D_FF = -(-8 * D_MODEL // (3 * 256)) * 256
D_IN = 4 * D_MLSTM + 2 * N_HEADS + D_POOL
MLSTM_CHUNK = 64
EPS = 1e-6

kernel_name = 'hymba_mlstm_pool_adaln_step'


def rmsnorm(x, g):
    xf = x.astype(jnp.float32)
    y = xf * lax.rsqrt(jnp.mean(xf * xf, axis=-1, keepdims=True) + EPS)
    return (y * g.astype(jnp.float32)).astype(x.dtype)


def mlstm_chunkwise(q, k, v, ig, lf, C0, n0, m0):
    B, T, H, Dh = q.shape
    L = math.gcd(T, MLSTM_CHUNK)
    nc = T // L

    def heads_first(a):
        a = a.reshape((B, nc, L) + a.shape[2:])
        return jnp.moveaxis(jnp.moveaxis(a, 1, 0), 2, 3)

    causal = jnp.tril(jnp.ones((L, L), dtype=bool))

    def step(carry, inp):
        C, n, m = carry
        qc, kc, vc, ic, fc = inp
        b = jnp.cumsum(fc, axis=-1)
        D = b[..., :, None] - b[..., None, :] + ic[..., None, :]
        D = jnp.where(causal, D, -jnp.inf)
        inter = b + m[..., None]
        m_t = jnp.maximum(inter, jnp.max(D, axis=-1))
        W = jnp.exp(D - m_t[..., None])
        a = jnp.exp(inter - m_t)
        S = jnp.einsum('bhtd,bhsd->bhts', qc, kc) * W
        num = a[..., None] * jnp.einsum('bhvd,bhtd->bhtv', C, qc) + jnp.einsum('bhts,bhsv->bhtv', S, vc)
        den = a * jnp.einsum('bhd,bhtd->bht', n, qc) + jnp.sum(S, axis=-1)
        h = num / jnp.maximum(jnp.abs(den), jnp.exp(-m_t))[..., None]
        m_new = m_t[..., -1]
        decay = jnp.exp(b[..., -1] + m - m_new)
        w = jnp.exp(b[..., -1:] - b + ic - m_new[..., None])
        C_new = decay[..., None, None] * C + jnp.einsum('bhs,bhsv,bhsd->bhvd', w, vc, kc)
        n_new = decay[..., None] * n + jnp.einsum('bhs,bhsd->bhd', w, kc)
        return (C_new, n_new, m_new), h

    xs = (heads_first(q), heads_first(k), heads_first(v), heads_first(ig), heads_first(lf))
    (C, n, m), h = lax.scan(step, (C0, n0, m0), xs)
    h = jnp.moveaxis(jnp.moveaxis(h, 3, 2), 0, 1).reshape(B, T, H, Dh)
    return h, C, n, m


def pool_mix(u, prev, pos0, w_pool, pool_scale):
    B, T, _ = u.shape
    ext = jnp.concatenate([prev.astype(u.dtype), u], axis=1).astype(jnp.float32)
    cs = jnp.concatenate([jnp.zeros((B, 1, D_POOL), jnp.float32), jnp.cumsum(ext, axis=1)], axis=1)
    pos = pos0 + jnp.arange(T)
    outs = []
    for g, w in enumerate(POOL_WINDOWS):
        sl = slice(g * POOL_GROUP, (g + 1) * POOL_GROUP)
        win = cs[:, POOL_BUF + 1:POOL_BUF + 1 + T, sl] - cs[:, POOL_BUF + 1 - w:POOL_BUF + 1 - w + T, sl]
        cnt = jnp.minimum(pos + 1, w).astype(jnp.float32)
        outs.append(win / cnt[None, :, None])
    pooled = jnp.concatenate(outs, axis=-1) - ext[:, POOL_BUF:]
    mixed = jnp.einsum('btgc,gcd->btgd', pooled.reshape(B, T, N_POOL_GROUPS, POOL_GROUP),
                       w_pool.astype(jnp.float32)).reshape(B, T, D_POOL)
    out = mixed * pool_scale.astype(jnp.float32)
    return out.astype(u.dtype), ext[:, -POOL_BUF:].astype(u.dtype)


def trunk_layer(x, c, C0, n0, m0, buf0, pos0, w_ada, b_ada, g_mix, w_in, b_gate, g_head,
                w_pool, pool_scale, w_out, g_ffn, w_ffn_in, w_ffn_out):
    B, T, _ = x.shape
    f32 = jnp.float32
    ada = jax.nn.silu(c) @ w_ada + b_ada
    sh1, sc1, gt1, sh2, sc2, gt2 = [a[:, None, :] for a in jnp.split(ada, 6, axis=-1)]
    h = rmsnorm(x, g_mix) * (1 + sc1) + sh1
    p = h @ w_in
    q, k, v, o, gates, u = jnp.split(
        p, [D_MLSTM, 2 * D_MLSTM, 3 * D_MLSTM, 4 * D_MLSTM, 4 * D_MLSTM + 2 * N_HEADS], axis=-1)
    gates = gates.astype(f32) + b_gate.astype(f32)
    ig = gates[..., :N_HEADS]
    lf = jax.nn.log_sigmoid(gates[..., N_HEADS:])
    shp = (B, T, N_HEADS, HEAD_DIM)
    hm, C, n, m = mlstm_chunkwise(
        q.reshape(shp).astype(f32), k.reshape(shp).astype(f32) * HEAD_DIM ** -0.5,
        v.reshape(shp).astype(f32), ig, lf, C0.astype(f32), n0.astype(f32), m0.astype(f32))
    hm = hm * lax.rsqrt(jnp.mean(hm * hm, axis=-1, keepdims=True) + EPS)
    hm = hm * g_head.reshape(N_HEADS, HEAD_DIM).astype(f32)
    hm = hm.reshape(B, T, D_MLSTM).astype(x.dtype) * jax.nn.sigmoid(o)
    hp, buf = pool_mix(u, buf0, pos0, w_pool, pool_scale)
    x = x + gt1 * (jnp.concatenate([hm, hp], axis=-1) @ w_out)
    h2 = rmsnorm(x, g_ffn) * (1 + sc2) + sh2
    a, bg = jnp.split(h2 @ w_ffn_in, 2, axis=-1)
    x = x + gt2 * ((jax.nn.silu(a) * bg) @ w_ffn_out)
    return x, C.astype(C0.dtype), n.astype(n0.dtype), m.astype(m0.dtype), buf.astype(buf0.dtype)


def setup_inputs(seed: int = 0) -> dict:
    key = jax.random.key(seed)
    ks = jax.random.split(key, 24)
    f32 = jnp.float32

    def nrm(k, shape, s):
        return jax.random.normal(k, shape, f32) * s

    b_gate = jnp.concatenate([-1.0 + nrm(ks[0], (DEPTH, N_HEADS), 0.1),
                              3.0 + nrm(ks[1], (DEPTH, N_HEADS), 0.5)], axis=-1)
    return {
        'x_prompt': nrm(ks[2], (BATCH, SEQ, D_MODEL), 1.0),
        'x_sample': nrm(ks[3], (DEC_BATCH, DEC_SEQ, D_MODEL), 1.0),
        'state_mlstm_C': nrm(ks[4], (DEPTH, DEC_BATCH, N_HEADS, HEAD_DIM, HEAD_DIM), 0.3),
        'state_mlstm_n': nrm(ks[5], (DEPTH, DEC_BATCH, N_HEADS, HEAD_DIM), 0.1),
        'state_mlstm_m': nrm(ks[6], (DEPTH, DEC_BATCH, N_HEADS), 0.5),
        'state_pool_buf': nrm(ks[7], (DEPTH, DEC_BATCH, POOL_BUF, D_POOL), 1.0),
        'c_prompt': nrm(ks[8], (BATCH, D_MODEL), 1.0),
        'c_sample': nrm(ks[9], (DEC_BATCH, D_MODEL), 1.0),
        'w_ada': nrm(ks[10], (DEPTH, D_MODEL, 6 * D_MODEL), 0.5 * D_MODEL ** -0.5),
        'b_ada': nrm(ks[11], (DEPTH, 6 * D_MODEL), 0.02),
        'g_mix': 1.0 + nrm(ks[12], (DEPTH, D_MODEL), 0.02),
        'w_in': nrm(ks[13], (DEPTH, D_MODEL, D_IN), D_MODEL ** -0.5),
        'b_gate': b_gate,
        'g_head': 1.0 + nrm(ks[14], (DEPTH, D_MLSTM), 0.02),
        'w_pool': nrm(ks[15], (DEPTH, N_POOL_GROUPS, POOL_GROUP, POOL_GROUP), POOL_GROUP ** -0.5),
        'pool_scale': 1.0 + nrm(ks[16], (DEPTH, D_POOL), 0.1),
        'w_out': nrm(ks[17], (DEPTH, D_MODEL, D_MODEL), D_MODEL ** -0.5),
        'g_ffn': 1.0 + nrm(ks[18], (DEPTH, D_MODEL), 0.02),
        'w_ffn_in': nrm(ks[19], (DEPTH, D_MODEL, 2 * D_FF), D_MODEL ** -0.5),
        'w_ffn_out': nrm(ks[20], (DEPTH, D_FF, D_MODEL), D_FF ** -0.5),
        'g_final': 1.0 + nrm(ks[21], (D_MODEL,), 0.02),
    }


def reference(x_prompt, x_sample, state_mlstm_C, state_mlstm_n, state_mlstm_m, state_pool_buf,
              c_prompt, c_sample, w_ada, b_ada, g_mix, w_in, b_gate, g_head, w_pool, pool_scale,
              w_out, g_ffn, w_ffn_in, w_ffn_out, g_final):
    B = x_prompt.shape[0]
    yp, ys = x_prompt, x_sample
    Cp, np_, mp, bp, Cs, ns, ms, bs = [], [], [], [], [], [], [], []
    for l in range(DEPTH):
        lw = (w_ada[l], b_ada[l], g_mix[l], w_in[l], b_gate[l], g_head[l], w_pool[l],
              pool_scale[l], w_out[l], g_ffn[l], w_ffn_in[l], w_ffn_out[l])
        C0 = jnp.zeros((B, N_HEADS, HEAD_DIM, HEAD_DIM), state_mlstm_C.dtype)
        n0 = jnp.zeros((B, N_HEADS, HEAD_DIM), state_mlstm_n.dtype)
        m0 = jnp.zeros((B, N_HEADS), state_mlstm_m.dtype)
        buf0 = jnp.zeros((B, POOL_BUF, D_POOL), state_pool_buf.dtype)
        yp, c1, n1, m1, b1 = trunk_layer(yp, c_prompt, C0, n0, m0, buf0, 0, *lw)
        ys, c2, n2, m2, b2 = trunk_layer(ys, c_sample, state_mlstm_C[l], state_mlstm_n[l],
                                         state_mlstm_m[l], state_pool_buf[l], PAST_LEN, *lw)
        Cp.append(c1); np_.append(n1); mp.append(m1); bp.append(b1)
        Cs.append(c2); ns.append(n2); ms.append(m2); bs.append(b2)
    yp = rmsnorm(yp, g_final)
    ys = rmsnorm(ys, g_final)
    return (yp, ys, jnp.stack(Cp), jnp.stack(np_), jnp.stack(mp), jnp.stack(bp),
            jnp.stack(Cs), jnp.stack(ns), jnp.stack(ms), jnp.stack(bs))
```

```python
import contextlib
import numpy as np
import concourse.bass as bass
import concourse.mybir as mybir
from concourse.bass_utils import run_bass_kernel_spmd

F32 = mybir.dt.float32
BF16 = mybir.dt.bfloat16
AF = mybir.ActivationFunctionType
ALU = mybir.AluOpType

EPS = 1e-6
NK = 868
POOL_W = (2, 4, 8, 16)


class Buf:
    __slots__ = ("name", "writers", "readers", "excl")

    def __init__(self, name="", excl=False):
        self.name = name
        self.writers = {}
        self.readers = {}
        self.excl = excl


class DSem:
    __slots__ = ("name", "n", "sem")

    def __init__(self, name):
        self.name = name
        self.n = 0
        self.sem = None


class Op:
    __slots__ = ("fn", "waits", "sig", "dsem", "rank")

    def __init__(self, fn, waits, dsem):
        self.fn = fn
        self.waits = waits
        self.sig = False
        self.dsem = dsem
        self.rank = 0


class Prog:
    ENG = ("pe", "act", "dve", "pool", "sp")

    def __init__(self, nc):
        self.nc = nc
        self.ops = {e: [] for e in self.ENG}
        self.seen = {e: {} for e in self.ENG}
        self.last = {e: -1 for e in self.ENG}
        self.pending = {e: {} for e in self.ENG}
        self.dsems = []
        self.off = False

    def dsem(self, name):
        d = DSem(name)
        self.dsems.append(d)
        return d

    def barrier(self):
        if self.off:
            return
        ev = {}
        for e in self.ENG:
            if self.last[e] >= 0:
                ev[e] = self.last[e]
        for d in self.dsems:
            if d.n > 0:
                ev[d] = d.n
        for e in self.ENG:
            pe = self.pending[e]
            for s, n in ev.items():
                if pe.get(s, -1) < n:
                    pe[s] = n

    def _deps(self, eng, reads, writes, accum, own=None):
        own = own if own is not None else eng
        deps = dict(self.pending[eng])
        self.pending[eng] = {}

        def add(ev):
            for s, n in ev.items():
                if deps.get(s, -1) < n:
                    deps[s] = n

        for b in reads:
            add(b.writers)
            if b.excl:
                add({s_: n_ for s_, n_ in b.readers.items() if s_ != eng})
        for b in writes:
            add(b.readers)
            if not accum:
                add(b.writers)
            else:
                add({s_: n_ for s_, n_ in b.writers.items() if s_ != own or (own == eng and eng != "pe")})
        waits = []
        seen = self.seen[eng]
        for s, n in deps.items():
            if s == "pe" and eng == "pe":
                continue
            if seen.get(s, -1) >= n:
                continue
            seen[s] = n
            waits.append((s, n))
        return waits

    def op(self, eng, fn, reads=(), writes=(), accum=False):
        if self.off:
            return
        waits = self._deps(eng, reads, writes, accum)
        lst = self.ops[eng]
        idx = len(lst)
        lst.append(Op(fn, waits, None))
        self.last[eng] = idx
        for b in reads:
            if b.readers.get(eng, -1) < idx:
                b.readers[eng] = idx
        for b in writes:
            if accum:
                b.writers[eng] = idx
            else:
                b.writers = {eng: idx}
                b.readers = {}
        return idx

    def dma(self, eng, dsem, fn, reads=(), writes=(), accum=False):
        if self.off:
            return
        waits = self._deps(eng, reads, writes, accum, own=dsem)
        dsem.n += 1
        n = dsem.n
        self.ops[eng].append(Op(fn, waits, dsem))
        for b in reads:
            if b.readers.get(dsem, -1) < n:
                b.readers[dsem] = n
        for b in writes:
            if accum:
                b.writers[dsem] = n
            else:
                b.writers = {dsem: n}
                b.readers = {}

    def emit(self, final_eng="sp"):
        nc = self.nc
        fin = [(d, d.n) for d in self.dsems if d.n > 0]
        for e in self.ENG:
            for o in self.ops[e]:
                for s, n in o.waits:
                    if isinstance(s, str):
                        self.ops[s][n].sig = True
        for e in self.ENG:
            r = 0
            for o in self.ops[e]:
                if o.sig:
                    r += 1
                    o.rank = r
        with contextlib.ExitStack() as st:
            esem = {e: st.enter_context(nc.semaphore("s_" + e)) for e in self.ENG}
            for i, d in enumerate(self.dsems):
                if d.n > 0:
                    d.sem = st.enter_context(nc.semaphore("d%d_%s" % (i, d.name)))
            block = st.enter_context(nc.Block())
            regs = {"pe": block.tensor, "act": block.scalar, "dve": block.vector,
                    "pool": block.gpsimd, "sp": block.sync}

            def make(e):
                def body(eng):
                    for o in self.ops[e]:
                        for s, n in o.waits:
                            if isinstance(s, str):
                                eng.wait_ge(esem[s], self.ops[s][n].rank)
                            else:
                                eng.wait_ge(s.sem, 16 * n)
                        ins = o.fn(eng)
                        if o.dsem is not None:
                            ins.then_inc(o.dsem.sem, 16)
                        elif o.sig:
                            ins.then_inc(esem[e], 1)
                    if e == final_eng:
                        for d, n in fin:
                            eng.wait_ge(d.sem, 16 * n)
                return body

            for e in self.ENG:
                regs[e](make(e))


def build():
    nc = bass.Bass("TRN2", target_bir_lowering=False)

    def din(name, shape):
        return nc.dram_tensor(name, list(shape), F32, kind="ExternalInput").ap()

    def dout(name, shape):
        return nc.dram_tensor(name, list(shape), F32, kind="ExternalOutput").ap()

    xo = din("xo", [1024, 2048]); xpre = din("xpre", [1024, 2048]); xs = din("xs", [128, 2048])
    cst = din("cst", [17, 2048]); Cs = din("Cs", [64, 256, 256]); ns = din("ns", [64, 256])
    ms = din("ms", [16, 4]); bufs = din("bufs", [16, 15, 1024]); flagd = din("flag", [128, 1])
    Kd = din("K", [128, NK])
    w_ada = din("w_ada", [2048, 12288]); b_ada = din("b_ada", [12288]); g_mix = din("g_mix", [2048])
    w_in = din("w_in", [2048, 5128]); b_gate = din("b_gate", [8]); g_head = din("g_head", [1024])
    w_pool = din("w_pool", [4, 256, 256]); pool_scale = din("pool_scale", [1024])
    w_out = din("w_out", [2048, 2048]); g_ffn = din("g_ffn", [2048])
    w_ffn_in = din("w_ffn_in", [2048, 11264]); w_ffn_out = din("w_ffn_out", [5632, 2048])
    g_final = din("g_final", [2048])
    yo = dout("yo", [1024, 2048]); ys = dout("ys", [128, 2048])
    Cp = dout("Cp", [4, 256, 256]); npo = dout("npo", [4, 256]); mpo = dout("mpo", [4, 1])
    bufp = dout("bufp", [15, 1024])
    Cso = dout("Cso", [64, 256, 256]); nso = dout("nso", [64, 256]); mso = dout("mso", [16, 4])
    bufso = dout("bufso", [16, 15, 1024])
    DBG = False
    dbg = dout("dbg", [128, 24576]) if DBG else None
    dbgpos = [0]
    dbgmap = {}

    def dump(name, ap, rdbufs, parts=128):
        if not DBG or p.off:
            return
        n = ap.shape[1]
        c0 = dbgpos[0]
        dbgpos[0] += n
        dbgmap[name] = (c0, n, parts)
        dma("sp", one(), dbg[0:parts, c0:c0 + n], ap, rdbufs, [])

    x1s = nc.dram_tensor("x1s", [1152, 2048], F32, kind="Internal").ap()
    x2s = nc.dram_tensor("x2s", [1152, 2048], F32, kind="Internal").ap()

    st = contextlib.ExitStack()
    arena = st.enter_context(nc.sbuf_tensor("arena", [128, 105472], BF16))
    PS = [st.enter_context(nc.psum_tensor("ps%d" % i, [128, 512], F32)) for i in range(8)]
    PSB = [Buf("ps%d" % i, excl=True) for i in range(8)]
    p = Prog(nc)
    STOP = 99

    def chk(k):
        if k > STOP:
            p.off = True

    def V(off, shape, dt=BF16, parts=128):
        shape = list(shape)
        n = int(np.prod(shape))
        assert off % 4 == 0
        if dt == BF16:
            assert off + 2 * n <= 210944, (off, shape)
            ap = arena[0:parts, off // 2: off // 2 + n]
        else:
            assert off + 4 * n <= 210944, (off, shape)
            ap = arena[0:parts, off // 2: off // 2 + 2 * n].bitcast(F32)
        if len(shape) == 2:
            ap = ap.rearrange("p (a b) -> p a b", b=shape[1])
        elif len(shape) == 3:
            ap = ap.rearrange("p (a b c) -> p a b c", b=shape[1], c=shape[2])
        return ap

    def psb(i):
        return PS[i][:, :].bitcast(BF16)

    rot = [0]
    rotn = [6]

    def nps():
        i = rot[0]
        rot[0] = (i + 1) % rotn[0]
        return i

    def mm(out, lhsT, rhs, start, stop, rd, wr):
        p.op("pe", lambda e: e.matmul(out, lhsT=lhsT, rhs=rhs, start=start, stop=stop), reads=rd, writes=wr)

    def tr(out, in_, ident, rd, wr):
        p.op("pe", lambda e: e.transpose(out=out, in_=in_, identity=ident), reads=rd, writes=wr)

    def act(out, in_, func, rd, wr, accum=False, **kw):
        p.op("act", lambda e: e.activation(out=out, in_=in_, func=func, **kw), reads=rd, writes=wr, accum=accum)

    def tt(eng, out, in0, in1, op, rd, wr, accum=False):
        p.op(eng, lambda e: e.tensor_tensor(out=out, in0=in0, in1=in1, op=op), reads=rd, writes=wr, accum=accum)

    def ts(eng, out, in0, s1, s2, op0, op1, rd, wr, accum=False):
        if s2 is None:
            p.op(eng, lambda e: e.tensor_scalar(out=out, in0=in0, scalar1=s1, scalar2=None, op0=op0), reads=rd, writes=wr, accum=accum)
        else:
            p.op(eng, lambda e: e.tensor_scalar(out=out, in0=in0, scalar1=s1, scalar2=s2, op0=op0, op1=op1), reads=rd, writes=wr, accum=accum)

    def stt(out, in0, scalar, in1, op0, op1, rd, wr, accum=False):
        p.op("dve", lambda e: e.scalar_tensor_tensor(out=out, in0=in0, scalar=scalar, in1=in1, op0=op0, op1=op1), reads=rd, writes=wr, accum=accum)

    def cp(eng, out, in_, rd, wr, accum=False):
        if eng == "act":
            p.op("act", lambda e: e.copy(out=out, in_=in_), reads=rd, writes=wr, accum=accum)
        else:
            p.op(eng, lambda e: e.tensor_copy(out=out, in_=in_), reads=rd, writes=wr, accum=accum)

    def memset(eng, ap, val, wr, accum=False):
        p.op(eng, lambda e: e.memset(ap, val), writes=wr, accum=accum)

    def recip(out, in_, rd, wr, accum=False):
        p.op("dve", lambda e: e.reciprocal(out=out, in_=in_), reads=rd, writes=wr, accum=accum)

    def scan(out, d0, d1, init, op0, op1, rd, wr):
        p.op("dve", lambda e: e.tensor_tensor_scan(out=out, data0=d0, data1=d1, initial=init, op0=op0, op1=op1), reads=rd, writes=wr)

    def dma(q, ds, out, in_, rd, wr, accum=False, nc_ok=False):
        if nc_ok:
            p.dma(q, ds, lambda e: e.dma_start(out=out, in_=in_, allow_slow_non_contiguous=True), reads=rd, writes=wr, accum=accum)
        else:
            p.dma(q, ds, lambda e: e.dma_start(out=out, in_=in_), reads=rd, writes=wr, accum=accum)

    O_K = 0
    O_IDB = 3584; O_MCB = 3840; O_MBB = 4096
    O_ADA = 4352
    O_SC1 = 10880; O_SC2 = 11968
    O_VEC = 13056
    O_SCT = 13632
    O_TOK = 14176
    O_DBP = 15264; O_DBO = 15392; O_DBS = 15520; O_DSJ = 15776
    O_MISC = 15808
    O_WGB = 16448; O_WGF = 16704
    O_W = [17408, 33792, 50176]
    O_XT = [66560, 74752]; O_XN = 82944
    P0 = 87040
    O_HTP = 87040; O_HTO = 119808
    O_ROW = 156672

    Ksb = V(O_K, [NK], F32)
    ident_f = Ksb[:, 0:128]
    BD16 = Ksb[:, 384:400]; E4 = Ksb[0:4, 400:404]; sellast = Ksb[:, 404:420]
    invtab = Ksb[:, 420:484]; segmul = Ksb[0:4, 484:612]; segadd = Ksb[0:4, 612:740]
    ones4 = Ksb[0:4, 740:868]
    ident_b = V(O_IDB, [128]); maskC_b = V(O_MCB, [128]); maskBD_b = V(O_MBB, [128])
    adaT = V(O_ADA, [96, 17], F32)
    scale1T = V(O_SC1, [16, 17], F32); scale2T = V(O_SC2, [16, 17], F32)
    vecT = V(O_VEC, [144], F32)
    b_adaT = vecT[:, 0:96]; g_mixT = vecT[:, 96:112]; g_ffnT = vecT[:, 112:128]
    g_headT = vecT[:, 128:136]; pool_scaleT = vecT[:, 136:144]
    scT = V(O_SCT, [16, 17])
    tok = V(O_TOK, [17, 16], F32)
    dbp = V(O_DBP, [32], F32); dbo = V(O_DBO, [32], F32); dbs = V(O_DBS, [64], F32)
    dsj = V(O_DSJ, [4], F32, parts=16)
    misc = V(O_MISC, [160], F32)
    wg_b = V(O_WGB, [16, 8]); wg_f = V(O_WGF, [16, 8], F32)
    Wv = [V(o, [16, 512]) for o in O_W]
    xt = [V(o, [2048], F32) for o in O_XT]
    xn = V(O_XN, [2048])
    xn_main = xn
    hT_pre = V(O_HTP, [16, 1024]); hT_own = V(O_HTO, [16, 1152])

    flag = misc[:, 0:1]; epsc = misc[:, 1:2]
    bi = misc[0:4, 2:3]; bfn = misc[0:4, 3:4]; nbf = misc[0:4, 4:5]
    mpre = misc[0:4, 5:6]; m_in = misc[0:4, 6:7]; mfo = misc[0:4, 7:8]
    Gs = misc[0:4, 16:32]; Ge = misc[0:4, 32:48]; dec = misc[0:4, 48:64]; mfs = misc[0:4, 64:80]
    msr = misc[0:4, 80:96]; tmpg = misc[0:4, 96:112]
    ssq = misc[:, 112:120]
    sml = misc[:, 120:160]

    B = {}

    def b(name):
        if name not in B:
            B[name] = Buf(name)
        return B[name]

    WB = [b("W0"), b("W1"), b("W2")]
    wds = [p.dsem("w0"), p.dsem("w1"), p.dsem("w2")]
    wrot = [0]

    def wslot():
        i = wrot[0]
        wrot[0] = (i + 1) % 3
        return i

    onecnt = [0]

    def one():
        onecnt[0] += 1
        return p.dsem("one%d" % onecnt[0])
    dxs = [p.dsem("x0"), p.dsem("x1")]
    dxs_st = [p.dsem("xs0"), p.dsem("xs1")]
    dout_s = [p.dsem("o%d" % i) for i in range(4)]
    orot = [0]

    def ods():
        i = orot[0]
        orot[0] = (i + 1) % 4
        return dout_s[i]

    dma("sp", one(), Ksb, Kd, [], [b("K")])
    dma("sp", one(), flag, flagd, [], [b("flag")])
    cs_f = V(O_XT[0], [2048], F32, parts=17)
    cs_b = V(O_XN, [2048], BF16, parts=17)
    dma("sp", one(), cs_f, cst, [], [b("xt0")])
    vr1 = V(O_XT[1], [128], F32, parts=96)
    vr2 = V(O_XT[1] + 512, [128], F32, parts=48)
    dma("sp", one(), vr1, b_ada.rearrange("(c p) -> c p", p=128), [], [b("xt1")])
    dma("sp", one(), vr2[0:16, :], g_mix.rearrange("(c p) -> c p", p=128), [], [b("xt1")], accum=True)
    dma("sp", one(), vr2[16:32, :], g_ffn.rearrange("(c p) -> c p", p=128), [], [b("xt1")], accum=True)
    dma("sp", one(), vr2[32:40, :], g_head.rearrange("(c p) -> c p", p=128), [], [b("xt1")], accum=True)
    dma("sp", one(), vr2[40:48, :], pool_scale.rearrange("(c p) -> c p", p=128), [], [b("xt1")], accum=True)
    dma("sp", one(), wg_f, w_in[:, 4096:4104].rearrange("(k p) n -> p k n", p=128), [], [b("wgf")], nc_ok=True)
    dma("sp", one(), bi, b_gate[0:4].rearrange("(p o) -> p o", o=1), [], [b("bi")], nc_ok=True)
    dma("sp", one(), bfn, b_gate[4:8].rearrange("(p o) -> p o", o=1), [], [b("bfn")], nc_ok=True)
    dma("sp", one(), msr, ms.rearrange("j h -> h j"), [], [b("msr")], nc_ok=True)

    cp("dve", ident_b, ident_f, [b("K")], [b("idb")])
    cp("dve", maskC_b, Ksb[:, 128:256], [b("K")], [b("mcb")])
    cp("dve", maskBD_b, Ksb[:, 256:384], [b("K")], [b("mbb")])
    memset("dve", epsc, EPS, [b("eps")])
    cp("dve", wg_b, wg_f, [b("wgf")], [b("wgb")])
    ts("dve", nbf, bfn, -1.0, None, ALU.mult, None, [b("bfn")], [b("nbf")])

    act(cs_b, cs_f, AF.Silu, [b("xt0")], [b("xn")])
    i0 = 6
    for k in range(16):
        tr(psb(i0)[0:128, k * 32: k * 32 + 17], cs_b[0:17, k * 128:(k + 1) * 128], ident_b[0:17, 0:17],
           [b("xn"), b("idb")], [PSB[i0]])
    cp("dve", scT, psb(i0)[:, 0:512].rearrange("p (k c) -> p k c", c=32)[:, :, 0:17], [PSB[i0]], [b("scT")])
    i1 = 7
    tr(PS[i1][:, 0:96], vr1[0:96, :], ident_f[0:96, 0:96], [b("xt1"), b("K")], [PSB[i1]])
    tr(PS[i1][:, 96:144], vr2[0:48, :], ident_f[0:48, 0:48], [b("xt1"), b("K")], [PSB[i1]])
    cp("dve", vecT, PS[i1][:, 0:144], [PSB[i1]], [b("vecT")])

    chk(1)
    def wload(dst, src_cols_ap, si, accum=False):
        dma("pool", wds[si], dst, src_cols_ap.rearrange("(k p) n -> p k n", p=128), [], [WB[si]], accum=accum)

    def ada_load(bi_, si):
        wload(Wv[si], w_ada[:, bi_ * 512:(bi_ + 1) * 512], si)

    def ada_mm(bi_, si):
        pi = nps()
        for cc in range(4):
            for k in range(16):
                mm(PS[pi][:, cc * 32: cc * 32 + 17], Wv[si][:, k, cc * 128:(cc + 1) * 128], scT[:, k, :],
                   k == 0, k == 15, [WB[si], b("scT")], [PSB[pi]])
        tt("dve", adaT[:, bi_ * 4:(bi_ + 1) * 4, :],
           PS[pi][:, 0:128].rearrange("p (c n) -> p c n", n=32)[:, :, 0:17],
           b_adaT[:, bi_ * 4:(bi_ + 1) * 4].unsqueeze(2).to_broadcast([128, 4, 17]), ALU.add,
           [PSB[pi], b("vecT")], [b("ada%d" % (bi_ // 4))], accum=True)

    def ada_block(bi_):
        si = wslot()
        ada_load(bi_, si)
        ada_mm(bi_, si)

    for bi_ in (4, 5, 6, 7, 0, 1, 2, 3):
        ada_block(bi_)
    stt(scale1T, adaT[:, 16:32, :], 1.0, g_mixT.unsqueeze(2).to_broadcast([128, 16, 17]), ALU.add, ALU.mult,
        [b("ada1"), b("vecT")], [b("sc1")])
    ada_rest = list(range(8, 24))

    chk(2)
    xn_alt = [None]

    def norm_to_T(ti, src_tile_buf, xtile, dstT, col0, scaleT, shift_c0, seqcol, sample, scb, shb, dstbuf, tmp4k):
        if xn_alt[0] is not None and ti % 2 == 1:
            xn, xnb = xn_alt[0], b("xn2")
        else:
            xn, xnb = xn_main, b("xn")
        s0 = ssq[:, (ti % 2) * 4 + 0:(ti % 2) * 4 + 1]
        s1 = ssq[:, (ti % 2) * 4 + 1:(ti % 2) * 4 + 2]
        s2 = ssq[:, (ti % 2) * 4 + 2:(ti % 2) * 4 + 3]
        sb_ = b("ssq%d" % (ti % 2))
        act(xn, xtile, AF.Square, [src_tile_buf], [xnb, sb_], accum_out=s0)
        act(s1, s0, AF.Sqrt, [sb_, b("eps")], [sb_], accum=True, scale=1.0 / 2048.0, bias=epsc)
        recip(s2, s1, [sb_], [sb_], accum=True)
        act(xn, xtile, AF.Copy, [src_tile_buf, sb_], [xnb], scale=s2)
        pa, pb_ = nps(), nps()
        for k in range(16):
            pi = pa if k < 8 else pb_
            tr(psb(pi)[:, (k % 8) * 128:(k % 8 + 1) * 128], xn[:, k * 128:(k + 1) * 128], ident_b,
               [xnb, b("idb")], [PSB[pi]])
        if not sample:
            for k in range(16):
                pi = pa if k < 8 else pb_
                src = psb(pi)[:, (k % 8) * 128:(k % 8 + 1) * 128]
                dst = dstT[:, k, col0:col0 + 128]
                if k < 8:
                    act(dst, src, AF.Identity, [PSB[pi], scb, shb], [dstbuf], accum=True,
                        scale=scaleT[:, k, seqcol:seqcol + 1], bias=adaT[:, shift_c0 + k, seqcol:seqcol + 1])
                else:
                    ts("dve", dst, src, scaleT[:, k, seqcol:seqcol + 1], adaT[:, shift_c0 + k, seqcol:seqcol + 1],
                       ALU.mult, ALU.add, [PSB[pi], scb, shb], [dstbuf], accum=True)
        else:
            for hb in range(2):
                pi = pa if hb == 0 else pb_
                k0 = hb * 8
                tmp = V(tmp4k, [8, 16, 8], F32)
                tt("dve", tmp, psb(pi)[:, :].rearrange("p (k j t) -> p k j t", j=16, t=8),
                   scaleT[:, k0:k0 + 8, 1:17].unsqueeze(3).to_broadcast([128, 8, 16, 8]), ALU.mult,
                   [PSB[pi], scb], [b("xt1")])
                tt("dve", dstT[:, k0:k0 + 8, col0:col0 + 128].rearrange("p k (j t) -> p k j t", t=8), tmp,
                   adaT[:, shift_c0 + k0:shift_c0 + k0 + 8, 1:17].unsqueeze(3).to_broadcast([128, 8, 16, 8]), ALU.add,
                   [b("xt1"), shb], [dstbuf], accum=True)

    def hbuf(ti):
        return b("hT%d" % ti)

    xn_alt[0] = V(O_ROW + 17408, [2048])
    for ti in range(17):
        if ti < 8:
            src = xpre[ti * 128:(ti + 1) * 128, :]; dstT = hT_pre; col0 = ti * 128
        elif ti < 16:
            src = xo[(ti - 8) * 128:(ti - 7) * 128, :]; dstT = hT_own; col0 = (ti - 8) * 128
        else:
            src = xs; dstT = hT_own; col0 = 1024
        s = ti % 2
        dma("sp", dxs[s], xt[s], src, [], [b("xt%d" % s)])
        norm_to_T(ti, b("xt%d" % s), xt[s], dstT, col0, scale1T, 0, 0, ti == 16, b("sc1"), b("ada0"),
                  hbuf(ti), O_XT[1])
        if ti % 4 == 2:
            ada_block(ada_rest.pop(0))
    xn_alt[0] = None

    chk(3)
    R_IG = O_ROW; R_LF = O_ROW + 8704; R0 = O_ROW + 17408

    def rowt(i, n=1024):
        return V(R0 + 4096 * i, [n], F32, parts=4)

    ig = V(R_IG, [2176], F32, parts=4); lf = V(R_LF, [2176], F32, parts=4)
    Bc, Ev, Gv, T1, Wp, Wpp, Rv, En, ONES = [rowt(i) for i in range(9)]
    memset("dve", ONES, 1.0, [b("ones")])
    blocks = [(hT_pre, 0, 512, 0, range(0, 4)), (hT_pre, 512, 512, 512, range(4, 8)),
              (hT_own, 0, 512, 1024, range(8, 12)), (hT_own, 512, 512, 1536, range(12, 16)),
              (hT_own, 1024, 128, 2048, range(16, 17))]
    for (hb_, c0, n, ro, tis) in blocks:
        rd = [hbuf(t) for t in tis] + [b("wgb")]
        for part in range(2):
            pi = nps()
            for k in range(16):
                mm(PS[pi][0:4, 0:n], wg_b[:, k, part * 4:(part + 1) * 4], hb_[:, k, c0:c0 + n], k == 0, k == 15,
                   rd, [PSB[pi]])
            if part == 0:
                act(ig[:, ro:ro + n], PS[pi][0:4, 0:n], AF.Identity, [PSB[pi], b("bi")], [b("ig")], accum=True, bias=bi)
            else:
                act(lf[:, ro:ro + n], PS[pi][0:4, 0:n], AF.Exp, [PSB[pi], b("nbf")], [b("lf")], accum=True, scale=-1.0, bias=nbf)
    act(lf, lf, AF.Ln, [b("lf")], [b("lf")], bias=1.0)
    ts("dve", lf, lf, -1.0, None, ALU.mult, None, [b("lf")], [b("lf")])

    rB, rE, rG, rT, rWp, rWpp, rR, rEn, rS = [b(n) for n in ("rB", "rE", "rG", "rT", "rWp", "rWpp", "rR", "rEn", "rS")]

    def seg_rows(c0, own):
        lfs = lf[:, c0:c0 + 1024]; igs = ig[:, c0:c0 + 1024]
        v3 = lambda a: a.rearrange("p (c t) -> p c t", t=128)
        scan(Bc, ONES, lfs, 0.0, ALU.mult, ALU.add, [b("ones"), b("lf")], [rB])
        tt("dve", Ev, igs, Bc, ALU.subtract, [rB, b("ig")], [rE])
        if own:
            scan(Gv, ONES, Ev, m_in, ALU.mult, ALU.max, [rE, b("ones"), b("m_in")], [rG])
            cp("dve", Gs[:, 0:1], m_in, [b("m_in")], [rS])
        else:
            scan(Gv, ONES, Ev, 0.0, ALU.mult, ALU.max, [rE, b("ones")], [rG])
            memset("dve", Gs[:, 0:1], 0.0, [rS])
        G3 = v3(Gv)
        cp("dve", Ge[:, 0:8], G3[:, :, 127], [rG], [rS], accum=True)
        cp("dve", Gs[:, 1:8], G3[:, 0:7, 127], [rG], [rS], accum=True)
        Geb = Ge[:, 0:8].unsqueeze(2).to_broadcast([4, 8, 128])
        Gsb = Gs[:, 0:8].unsqueeze(2).to_broadcast([4, 8, 128])
        tt("dve", v3(Wpp), v3(Ev), Geb, ALU.subtract, [rE, rS], [rWpp])
        act(Wpp, Wpp, AF.Exp, [rWpp], [rWpp])
        if own:
            tt("dve", v3(Wp), v3(Ev), Gsb, ALU.subtract, [rE, rS], [rWp])
            act(Wp, Wp, AF.Exp, [rWp], [rWp])
            tt("dve", v3(Rv), Gsb, v3(Gv), ALU.subtract, [rG, rS], [rR])
            act(Rv, Rv, AF.Exp, [rR], [rR])
            tt("dve", En, Bc, Gv, ALU.add, [rB, rG], [rEn])
            act(En, En, AF.Exp, [rEn], [rEn], scale=-1.0)
        tt("dve", tmpg[:, 0:8], Gs[:, 0:8], Ge[:, 0:8], ALU.subtract, [rS], [b("tmpg")])
        act(dec[:, 0:8], tmpg[:, 0:8], AF.Exp, [b("tmpg")], [b("dec")])
        tt("dve", mfo if own else mpre, Bc[:, 1023:1024], Gv[:, 1023:1024], ALU.add, [rB, rG], [b("mfo") if own else b("mpre")])
        pi = nps()
        t0 = 8 if own else 0
        qs = [(0, Wp, rWp), (1, Wpp, rWpp), (2, Rv, rR), (3, En, rEn)] if own else [(1, Wpp, rWpp)]
        for c in range(8):
            for q, rt, rb_ in qs:
                tr(PS[pi][:, c * 16 + q * 4: c * 16 + q * 4 + 4], rt[:, c * 128:(c + 1) * 128], ident_f[0:4, 0:4],
                   [rb_, b("K")], [PSB[pi]])
        pv = PS[pi][:, 0:128].rearrange("p (c q) -> p c q", q=16)
        if own:
            cp("dve", tok[:, t0:t0 + 8, :], pv, [PSB[pi]], [b("tok")], accum=True)
        else:
            cp("dve", tok[:, t0:t0 + 8, 4:8], pv[:, :, 4:8], [PSB[pi]], [b("tok")], accum=True)
        X = T1[:, 0:32].rearrange("p (h c) -> p h c", c=8)
        tt("dve", X, dec[:, 0:8].unsqueeze(1).to_broadcast([4, 4, 8]), E4.unsqueeze(2).to_broadcast([4, 4, 8]), ALU.mult,
           [b("dec"), b("K")], [rT])
        pj = nps()
        mm(PS[pj][:, 0:32], ones4, T1[:, 0:32], True, True, [rT, b("K")], [PSB[pj]])
        cp("dve", dbo if own else dbp, PS[pj][:, 0:32], [PSB[pj]], [b("dbo") if own else b("dbp")])

    pre_slots = []
    for c0 in (1024, 1536, 2048):
        si = wslot()
        wload(Wv[si], w_in[:, c0:c0 + 512], si)
        pre_slots.append(si)
    seg_rows(0, False)
    tt("dve", m_in, mpre, flag[0:4, :], ALU.mult, [b("mpre"), b("flag")], [b("m_in")])
    seg_rows(1024, True)
    dma("sp", ods(), mpo, mfo, [b("mfo")], [])

    def seg_sample():
        c0 = 2048
        lfs = lf[:, c0:c0 + 128]; igs = ig[:, c0:c0 + 128]
        B_, E_, G_, T_, Wp_, Wpp_, R_, En_ = [a[:, 0:128] for a in (Bc, Ev, Gv, T1, Wp, Wpp, Rv, En)]
        v3 = lambda a: a.rearrange("p (j t) -> p j t", t=8)
        scan(B_, segmul, lfs, 0.0, ALU.mult, ALU.add, [b("K"), b("lf")], [rB])
        tt("dve", E_, igs, B_, ALU.subtract, [rB, b("ig")], [rE])
        cp("dve", T_, E_, [rE], [rT])
        tt("dve", v3(T_)[:, :, 0], v3(E_)[:, :, 0], msr, ALU.max, [rE, b("msr")], [rT], accum=True)
        scan(G_, segadd, T_, 0.0, ALU.add, ALU.max, [rT, b("K")], [rG])
        cp("dve", Ge, v3(G_)[:, :, 7], [rG], [rS])
        Geb = Ge.unsqueeze(2).to_broadcast([4, 16, 8])
        Gsb = msr.unsqueeze(2).to_broadcast([4, 16, 8])
        tt("dve", v3(Wpp_), v3(E_), Geb, ALU.subtract, [rE, rS], [rWpp])
        act(Wpp_, Wpp_, AF.Exp, [rWpp], [rWpp])
        tt("dve", v3(Wp_), v3(E_), Gsb, ALU.subtract, [rE, b("msr")], [rWp])
        act(Wp_, Wp_, AF.Exp, [rWp], [rWp])
        tt("dve", v3(R_), Gsb, v3(G_), ALU.subtract, [rG, b("msr")], [rR])
        act(R_, R_, AF.Exp, [rR], [rR])
        tt("dve", En_, B_, G_, ALU.add, [rB, rG], [rEn])
        act(En_, En_, AF.Exp, [rEn], [rEn], scale=-1.0)
        tt("dve", tmpg, msr, Ge, ALU.subtract, [b("msr"), rS], [b("tmpg")])
        act(dec, tmpg, AF.Exp, [b("tmpg")], [b("dec")])
        tt("dve", mfs, v3(B_)[:, :, 7], v3(G_)[:, :, 7], ALU.add, [rB, rG], [b("mfs")])
        dma("sp", ods(), mso.rearrange("j h -> h j"), mfs, [b("mfs")], [], nc_ok=True)
        pi = nps()
        for q, rt, rb_ in [(0, Wp_, rWp), (1, Wpp_, rWpp), (2, R_, rR), (3, En_, rEn)]:
            tr(PS[pi][:, q * 4: q * 4 + 4], rt, ident_f[0:4, 0:4], [rb_, b("K")], [PSB[pi]])
        cp("dve", tok[:, 16, :], PS[pi][:, 0:16], [PSB[pi]], [b("tok")], accum=True)
        X = T1[:, 128:192].rearrange("p (h c) -> p h c", c=16)
        tt("dve", X, dec.unsqueeze(1).to_broadcast([4, 4, 16]), E4.unsqueeze(2).to_broadcast([4, 4, 16]), ALU.mult,
           [b("dec"), b("K")], [b("rT2")])
        pj = nps()
        mm(PS[pj][:, 0:64], ones4, T1[:, 128:192], True, True, [b("rT2"), b("K")], [PSB[pj]])
        cp("dve", dbs, PS[pj][:, 0:64], [PSB[pj]], [b("dbs")])
        pk = nps()
        mm(PS[pk][0:16, 0:4], sellast, tok[:, 16, 8:12], True, True, [b("K"), b("tok")], [PSB[pk]])
        cp("dve", dsj, PS[pk][0:16, 0:4], [PSB[pk]], [b("dsj")])

    dump("ig", ig, [b("ig")], 4)
    dump("lf", lf, [b("lf")], 4)
    seg_sample()
    dump("tok", tok.rearrange("p a b -> p (a b)"), [b("tok")])
    dump("dbp", dbp, [b("dbp")]); dump("dbo", dbo, [b("dbo")]); dump("dbs", dbs, [b("dbs")])
    dump("misc", misc, [b("mfs"), b("mfo"), b("mpre"), b("m_in"), b("dec"), b("msr")])
    dump("adaT", adaT.rearrange("p a b -> p (a b)"), [b("ada%d" % i) for i in range(6)])
    dump("sc1", scale1T.rearrange("p a b -> p (a b)"), [b("sc1")])
    dump("Bs", Bc[:, 0:128], [rB], 4); dump("Es", Ev[:, 0:128], [rE], 4); dump("Gs_", Gv[:, 0:128], [rG], 4)
    hp7 = V(84384, [16, 16])
    cp("dve", hp7, hT_pre[:, :, 1008:1024], [hbuf(7)], [b("hp7"), b("xn")])
    p.barrier()

    chk(4)
    O_KPRE = 156672; O_VPRE = 173056
    O_KW = 193536; O_CT = 195584; O_CTB = 203840
    kpre = V(O_KPRE, [8, 1024]); vpre = V(O_VPRE, [8, 4, 258])
    kw = [V(O_KW + 512 * i, [256]) for i in range(4)]
    CT = [V(O_CT + 2064 * h, [2, 258], F32) for h in range(4)]
    CTb = [V(O_CTB + 1032 * h, [2, 258]) for h in range(4)]
    memset("dve", vpre[:, :, :, 256:257], 1.0, [b("vpre")])
    for h in range(4):
        memset("dve", CT[h], 0.0, [b("CT%d" % h)])
    for bi_, c0 in enumerate((1024, 1536, 2048, 2560)):
        if bi_ < 3:
            si = pre_slots[bi_]
        else:
            si = wslot()
            wload(Wv[si], w_in[:, c0:c0 + 512], si)
        for t in range(8):
            pi = nps()
            for k in range(16):
                mm(PS[pi][:, :], hT_pre[:, k, t * 128:(t + 1) * 128], Wv[si][:, k, :], k == 0, k == 15,
                   [hbuf(t), WB[si]], [PSB[pi]])
            if bi_ < 2:
                ts("dve", kpre[:, t, bi_ * 512:(bi_ + 1) * 512], PS[pi][:, :], 0.0625, None, ALU.mult, None, [PSB[pi]],
                   [b("kpre%d" % t)], accum=True)
            else:
                vb = bi_ - 2
                cp("act" if t % 2 == 0 else "dve", vpre[:, t, 2 * vb:2 * vb + 2, 0:256],
                   PS[pi][:, :].rearrange("p (a c) -> p a c", c=256), [PSB[pi]], [b("vpre")], accum=True)
    def load_w(h, sa_, sv_):
        wload(Wv[sa_][:, :, 0:256], w_in[:, h * 256:(h + 1) * 256], sa_)
        wload(Wv[sa_][:, :, 256:512], w_in[:, 1024 + h * 256:1024 + (h + 1) * 256], sa_, accum=True)
        wload(Wv[sv_][:, :, 0:256], w_in[:, 2048 + h * 256:2048 + (h + 1) * 256], sv_)
        wload(Wv[sv_][:, :, 256:512], w_in[:, 3072 + h * 256:3072 + (h + 1) * 256], sv_, accum=True)

    s_a = wslot(); s_v = wslot()
    s_f = ({0, 1, 2} - {s_a, s_v}).pop()
    load_w(0, s_a, s_v)
    ada_cur = ada_rest.pop(0)
    ada_load(ada_cur, s_f)
    steps = [(c, h) for c in range(8) for h in range(4)]

    def KWp(i):
        c, h = steps[i]
        r = i % 4
        ts("dve", kw[r], kpre[:, c, h * 256:(h + 1) * 256], tok[:, c, 4 + h:5 + h], None, ALU.mult, None,
           [b("kpre%d" % c), b("tok")], [b("kw%d" % r)])

    for i in range(3):
        KWp(i)
    for i, (c, h) in enumerate(steps):
        r = i % 4
        pis = []
        for d_ in range(2):
            pi = nps()
            pis.append(pi)
            mm(PS[pi][:, 0:257], kw[r][:, d_ * 128:(d_ + 1) * 128], vpre[:, c, h, 0:257], True, True,
               [b("kw%d" % r), b("vpre")], [PSB[pi]])
        if i + 3 < len(steps):
            KWp(i + 3)
        for d_ in range(2):
            pi = pis[d_]
            stt(CT[h][:, d_, 0:257], CT[h][:, d_, 0:257], dbp[:, h * 8 + c:h * 8 + c + 1], PS[pi][:, 0:257],
                ALU.mult, ALU.add, [b("CT%d" % h), b("dbp"), PSB[pi]], [b("CT%d" % h)])
    for h in range(4):
        ts("dve", CT[h], CT[h], flag, None, ALU.mult, None, [b("CT%d" % h), b("flag")], [b("CT%d" % h)])
    p.barrier()

    chk(5)
    O_QT = 87040; O_KT = 91648; O_OTS = 96256; O_KTOK = 100864; O_VEXT = 105472
    O_SP = 110144; O_HMN = 110656; O_NTB = 111680; O_NIN = 111936; O_NROWS = 116032
    O_CAT = 156672
    O_QBD = 66560; O_CN = 74752; O_CTE = 80896; O_VW = 82976; O_WM = 84000
    qT = V(O_QT, [2, 1152]); kT = V(O_KT, [2, 1152]); oTs = V(O_OTS, [2, 1152])
    ktok = V(O_KTOK, [9, 256]); vext = V(O_VEXT, [9, 258])
    Sp = [V(O_SP + 256 * i, [128]) for i in range(2)]
    hmn = [V(O_HMN + 512 * i, [256]) for i in range(2)]
    nT_b = V(O_NTB, [2, 64])
    nin = V(O_NIN, [1024], F32, parts=16)
    nrows = V(O_NROWS, [256], F32, parts=64)
    catT = V(O_CAT, [16, 1152])
    QBD = V(O_QBD, [2, 2048])
    Cn = [V(O_CN + 2048 * i, [2, 256], F32) for i in range(3)] + [V(84896, [2, 256], F32), V(117056, [2, 256], F32)]
    NCN = 5
    CTe = [V(O_CTE + 1040 * i, [2, 258]) for i in range(2)] + [V(207968, [2, 258]), V(209008, [2, 258])]
    NCE = 4
    vw = [V(O_VW + 512 * i, [256]) for i in range(2)] + [V(210048, [256])]
    NVW = 3
    wm_b = V(O_WM, [4, 16])
    wm_f = V(O_WM + 128, [4, 16], F32)
    Sps = V(119104, [128])
    CTbp = [V(O_CTB + 1032 * i, [2, 258]) for i in range(2)]
    rotn[0] = 5; rot[0] = 0
    PSA_C = [5, 7]; PSA_S = 6

    memset("dve", QBD, 0.0, [b("QBD")])
    memset("dve", vext[:, :, 256:257], 1.0, [b("vext_ones")])
    dcs = [p.dsem("cn%d" % i) for i in range(NCN)]
    dcs_st = [p.dsem("cns%d" % i) for i in range(NCN)]
    dma("sp", one(), nrows, ns, [], [b("nrows")])
    dma("sp", one(), nin, ns.rearrange("(j h) k -> j (h k)", h=4), [], [b("nin")])
    pi = nps()
    for kc in range(2):
        tr(PS[pi][:, kc * 64:(kc + 1) * 64], nrows[0:64, kc * 128:(kc + 1) * 128], ident_f[0:64, 0:64],
           [b("nrows"), b("K")], [PSB[pi]])
    cp("dve", nT_b, PS[pi][:, 0:128].rearrange("p (a c) -> p a c", c=64), [PSB[pi]], [b("nTb")])
    tt("dve", wm_f, tok[:, 16, 4:8].unsqueeze(2).to_broadcast([128, 4, 16]),
       BD16.unsqueeze(1).to_broadcast([128, 4, 16]), ALU.mult, [b("tok"), b("K")], [b("wmf")])
    cp("dve", wm_b, wm_f, [b("wmf")], [b("wmb")])

    def diag_ap(base):
        return bass.AP(base.tensor, base.offset, [list(base.ap[0]), [136, 16], [1, 8]])

    def evac_chain(pa, t, h):
        sb_ = b("sml%d" % (t % 2))
        o_ = (t % 2) * 8
        c = lambda i: sml[:, o_ + i:o_ + i + 1]
        r_tok = tok[:, t, 8 + h:9 + h]; en_tok = tok[:, t, 12 + h:13 + h]
        r = t % 2
        act(c(0), PS[pa][:, 256:257], AF.Abs, [PSB[pa], b("tok")], [sb_], scale=r_tok)
        tt("dve", c(1), c(0), en_tok, ALU.max, [sb_, b("tok")], [sb_], accum=True)
        recip(c(2), c(1), [sb_], [sb_], accum=True)
        tt("dve", c(3), c(2), r_tok, ALU.mult, [sb_, b("tok")], [sb_], accum=True)
        act(hmn[r], PS[pa][:, 0:256], AF.Square, [PSB[pa], sb_], [b("hmn%d" % r), sb_], accum=True, scale=c(3), accum_out=c(4))
        act(c(6), c(4), AF.Sqrt, [sb_, b("eps")], [sb_], accum=True, scale=1.0 / 256.0, bias=epsc)
        recip(c(7), c(6), [sb_], [sb_], accum=True)
        tt("dve", c(7), c(7), c(3), ALU.mult, [sb_], [sb_], accum=True)
        ts("dve", hmn[r], PS[pa][:, 0:256], c(7), None, ALU.mult, None, [PSB[pa], sb_], [b("hmn%d" % r)])

    def evac_T(t, h, col0):
        r = t % 2
        pt = nps()
        for vc in range(2):
            tr(psb(pt)[:, vc * 128:(vc + 1) * 128], hmn[r][:, vc * 128:(vc + 1) * 128], ident_b,
               [b("hmn%d" % r), b("idb")], [PSB[pt]])
        for vc in range(2):
            stt(catT[:, h * 2 + vc, col0:col0 + 128], psb(pt)[:, vc * 128:(vc + 1) * 128],
                g_headT[:, h * 2 + vc:h * 2 + vc + 1], oTs[:, vc, col0:col0 + 128], ALU.mult, ALU.mult,
                [PSB[pt], b("vecT"), b("oTs")], [b("catT")], accum=True)

    def sample_gen(h):
        def load(j):
            ci = j % NCN
            dma("sp", dcs[ci], Cn[ci], Cs[j * 4 + h].rearrange("(vc p) k -> p vc k", p=128), [], [b("Cn%d" % ci)])

        def T(j):
            ci = j % NCN; e = j % NCE; pair = j * 4 + h
            pc = nps()
            for kc in range(2):
                for vc in range(2):
                    tr(PS[pc][:, kc * 256 + vc * 128: kc * 256 + (vc + 1) * 128], Cn[ci][:, vc, kc * 128:(kc + 1) * 128],
                       ident_f, [b("Cn%d" % ci), b("K")], [PSB[pc]])
            cp("act", CTe[e][:, :, 0:256], PS[pc][:, :].rearrange("p (a c) -> p a c", c=256), [PSB[pc]], [b("CTe%d" % e)])
            cp("dve", CTe[e][:, :, 256:257], nT_b[:, :, pair:pair + 1], [b("nTb")], [b("CTe%d" % e)], accum=True)

        for j in range(NCN):
            load(j)
        T(0)
        yield
        for j in range(16):
            ci = j % NCN; e = j % NCE; v_ = j % NVW; pair = j * 4 + h
            ts("dve", vw[v_], vext[:, 8, 0:256], wm_f[:, h, j:j + 1], None, ALU.mult, None, [b("vext8"), b("wmf")], [b("vw%d" % v_)])
            if j + 1 < 16:
                T(j + 1)
            for kc in range(2):
                mm(PS[PSA_S][:, 0:257], QBD[:, kc, j * 128:(j + 1) * 128], CTe[e][:, kc, 0:257], j == 0 and kc == 0, False,
                   [b("QBD"), b("CTe%d" % e)], [PSB[PSA_S]])
            pn = nps()
            for vc in range(2):
                mm(PS[pn][:, vc * 256:(vc + 1) * 256], vw[v_][:, vc * 128:(vc + 1) * 128], ktok[:, 8, :], True, True,
                   [b("vw%d" % v_), b("ktok8")], [PSB[pn]])
            stt(Cn[ci], Cn[ci], dbs[:, h * 16 + j:h * 16 + j + 1], PS[pn][:, :].rearrange("p (a c) -> p a c", c=256),
                ALU.mult, ALU.add, [b("Cn%d" % ci), b("dbs"), PSB[pn]], [b("Cn%d" % ci)])
            dma("sp", dcs_st[ci], Cso[pair].rearrange("(vc p) k -> p vc k", p=128), Cn[ci], [b("Cn%d" % ci)], [])
            if j + NCN < 16:
                load(j + NCN)
            yield

    dcpo = p.dsem("cpo")
    dnpo = p.dsem("npo")
    for h in range(4):
        sa, sv = s_a, s_v
        free_slot = s_f
        if h > 0:
            ada_cur = ada_rest.pop(0)
            ada_load(ada_cur, s_f)
        sg = sample_gen(h)
        next(sg)
        hall = [hbuf(t) for t in range(8, 17)]
        n_ev = 0
        for (dst, si, off, kind, dbuf) in ((qT, sa, 0, 0, "qT"), (kT, sa, 256, 0, "kT"), (oTs, sv, 256, 1, "oTs")):
            for cc in range(2):
                for tb in range(3):
                    pi = nps()
                    for k in range(16):
                        mm(PS[pi][:, 0:384], Wv[si][:, k, off + cc * 128: off + (cc + 1) * 128],
                           hT_own[:, k, tb * 384:(tb + 1) * 384], k == 0, k == 15, hall + [WB[si]], [PSB[pi]])
                    d = dst[:, cc, tb * 384:(tb + 1) * 384]
                    if kind == 1:
                        act(d, PS[pi][:, 0:384], AF.Sigmoid, [PSB[pi]], [b(dbuf)], accum=True)
                    else:
                        sc_ = 0.0625 if dbuf == "kT" else 1.0
                        if n_ev % 2 == 0:
                            act(d, PS[pi][:, 0:384], AF.Copy, [PSB[pi]], [b(dbuf)], accum=True, scale=sc_)
                        else:
                            ts("dve", d, PS[pi][:, 0:384], sc_, None, ALU.mult, None, [PSB[pi]], [b(dbuf)], accum=True)
                        n_ev += 1
        for t in range(9):
            pi = nps(); pv_ = nps()
            for k in range(16):
                mm(PS[pi][:, 0:256], hT_own[:, k, t * 128:(t + 1) * 128], Wv[sa][:, k, 256:512], k == 0, k == 15,
                   [hbuf(8 + t), WB[sa]], [PSB[pi]])
            for k in range(16):
                mm(PS[pv_][:, 0:256], hT_own[:, k, t * 128:(t + 1) * 128], Wv[sv][:, k, 0:256], k == 0, k == 15,
                   [hbuf(8 + t), WB[sv]], [PSB[pv_]])
            act(ktok[:, t, :], PS[pi][:, 0:256], AF.Copy, [PSB[pi]], [b("ktok%d" % t)], scale=0.0625)
            cp("dve", vext[:, t, 0:256], PS[pv_][:, 0:256], [PSB[pv_]], [b("vext%d" % t)])
        ada_mm(ada_cur, s_f)
        ada_cur = ada_rest.pop(0)
        ada_load(ada_cur, s_f)
        if h < 3:
            load_w(h + 1, s_a, s_v)
        else:
            su = [s_a, s_v]
            wload(Wv[su[0]], w_in[:, 4104:4616], su[0])
            wload(Wv[su[1]], w_in[:, 4616:5128], su[1])
        scol = slice(1024, 1152)
        cp("act", CTbp[0], CT[h], [b("CT%d" % h)], [b("CTbp0")])
        ps_ = nps()
        for d_ in range(2):
            mm(PS[ps_][:, 0:128], kT[:, d_, scol], qT[:, d_, scol], d_ == 0, d_ == 1, [b("kT"), b("qT")], [PSB[ps_]])
        stt(Sps, PS[ps_][:, 0:128], tok[:, 16, h:h + 1], maskBD_b, ALU.mult, ALU.mult,
            [PSB[ps_], b("tok"), b("mbb")], [b("Sps")])
        for d_ in range(2):
            cp("dve", diag_ap(QBD[:, d_, :]), qT[:, d_, scol].rearrange("p (j t) -> p j t", t=8), [b("qT")], [b("QBD")],
               accum=(d_ == 1))
        def ST(c):
            cols = slice(c * 128, (c + 1) * 128)
            ps2 = nps()
            for d_ in range(2):
                mm(PS[ps2][:, 0:128], kT[:, d_, cols], qT[:, d_, cols], d_ == 0, d_ == 1, [b("kT"), b("qT")], [PSB[ps2]])
            stt(Sp[c % 2], PS[ps2][:, 0:128], tok[:, 8 + c, h:h + 1], maskC_b, ALU.mult, ALU.mult,
                [PSB[ps2], b("tok"), b("mcb")], [b("Sp%d" % (c % 2))])

        def KW(c):
            rk = c % 4
            ts("dve", kw[rk], ktok[:, c, :], tok[:, 8 + c, 4 + h:5 + h], None, ALU.mult, None,
               [b("ktok%d" % c), b("tok")], [b("kw%d" % rk)])

        ST(0); KW(0)
        for c in range(8):
            t = 8 + c
            cols = slice(c * 128, (c + 1) * 128)
            pa = PSA_C[c % 2]
            cb_ = b("CTbp%d" % (c % 2))
            for d_ in range(2):
                mm(PS[pa][:, 0:257], qT[:, d_, cols], CTbp[c % 2][:, d_, 0:257], d_ == 0, False, [b("qT"), cb_], [PSB[pa]])
            mm(PS[pa][:, 0:257], Sp[c % 2], vext[:, c, 0:257], False, True, [b("Sp%d" % (c % 2)), b("vext%d" % c), b("vext_ones")], [PSB[pa]])
            rk = c % 4
            for d_ in range(2):
                pu = nps()
                mm(PS[pu][:, 0:257], kw[rk][:, d_ * 128:(d_ + 1) * 128], vext[:, c, 0:257], True, True,
                   [b("kw%d" % rk), b("vext%d" % c), b("vext_ones")], [PSB[pu]])
                stt(CT[h][:, d_, 0:257], CT[h][:, d_, 0:257], dbo[:, h * 8 + c:h * 8 + c + 1], PS[pu][:, 0:257],
                    ALU.mult, ALU.add, [b("CT%d" % h), b("dbo"), PSB[pu]], [b("CT%d" % h)])
            if c < 7:
                cp("act", CTbp[(c + 1) % 2], CT[h], [b("CT%d" % h)], [b("CTbp%d" % ((c + 1) % 2))])
                ST(c + 1); KW(c + 1)
            evac_chain(pa, t, h)
            if c > 0:
                evac_T(t - 1, h, (c - 1) * 128)
            for _ in range(2):
                next(sg, None)
            if c == 3:
                ada_mm(ada_cur, s_f)
                ada_cur = ada_rest.pop(0)
                ada_load(ada_cur, s_f)
            if c == 7:
                ada_mm(ada_cur, s_f)
        evac_T(15, h, 7 * 128)
        po = nps()
        for d_ in range(2):
            for vc in range(2):
                tr(PS[po][:, vc * 256 + d_ * 128: vc * 256 + (d_ + 1) * 128], CT[h][:, d_, vc * 128:(vc + 1) * 128],
                   ident_f, [b("CT%d" % h), b("K")], [PSB[po]])
        cpo = V(O_KW, [2, 256], F32)
        cp("dve", cpo, PS[po][:, :].rearrange("p (a c) -> p a c", c=256), [PSB[po]], [b("kw0"), b("kw1"), b("kw2"), b("kw3")])
        dma("pool", dcpo, Cp[h].rearrange("(vc p) d -> p vc d", p=128), cpo, [b("kw0"), b("kw1"), b("kw2"), b("kw3")], [])
        dma("pool", dnpo, npo[h].rearrange("(d p o) -> p d o", p=128, o=1), CT[h][:, :, 256:257], [b("CT%d" % h)], [], nc_ok=True)
        for _ in sg:
            pass
        mm(PS[PSA_S][:, 0:257], Sps, vext[:, 8, 0:257], False, True, [b("Sps"), b("vext8"), b("vext_ones")], [PSB[PSA_S]])
        evac_chain(PSA_S, 16, h)
        evac_T(16, h, 1024)
        pq = nps()
        mm(PS[pq][0:16, 0:256], wm_b[:, h, :], ktok[:, 8, :], True, True, [b("wmb"), b("ktok8")], [PSB[pq]])
        stt(nin[:, h * 256:(h + 1) * 256], nin[:, h * 256:(h + 1) * 256], dsj[:, h:h + 1], PS[pq][0:16, 0:256],
            ALU.mult, ALU.add, [b("nin"), b("dsj"), PSB[pq]], [b("nin")])
    rotn[0] = 6; rot[0] = 0
    assert not ada_rest
    stt(scale2T, adaT[:, 64:80, :], 1.0, g_ffnT.unsqueeze(2).to_broadcast([128, 16, 17]), ALU.add, ALU.mult,
        [b("ada4"), b("vecT")], [b("sc2")])
    dma("sp", ods(), nso.rearrange("(j h) k -> j (h k)", h=4), nin, [b("nin")], [])
    p.barrier()

    chk(6)
    O_UEXT = 87040; O_USX = 91648; O_PA = 93184; O_PB = 97792; O_PSA = 102400; O_PSB_ = 103936
    O_POOLED = 105472; O_UTOK = 110080; O_UTOKS = 114176; O_USC = 118272
    O_BR = 66560; O_WP = 74752 + 8192 - 8192
    uext = V(O_UEXT, [1152], F32); usx = V(O_USX, [16, 23], F32)
    pa_ = V(O_PA, [1152], F32); pb2 = V(O_PB, [1152], F32)
    psa = V(O_PSA, [16, 23], F32); psb2 = V(O_PSB_, [16, 23], F32)
    pooled = V(O_POOLED, [2, 1152])
    utok = V(O_UTOK, [1024], F32); utoks = V(O_UTOKS, [1024], F32)
    usc = V(O_USC, [128], F32)
    br = [V(O_BR + 4096 * i, [1024], F32, parts=120) for i in range(2)]
    wp = V(O_BR + 8192, [4, 2, 256])
    for i in range(2):
        dma("sp", one(), br[i], bufs[i * 8:(i + 1) * 8].rearrange("j r c -> (j r) c"), [], [b("br%d" % i)])
    dwp = p.dsem("wp")
    for g in range(4):
        dma("pool", dwp, wp[:, g, :, :], w_pool[g].rearrange("(cc p) d -> p cc d", p=128), [], [b("wp")], accum=True)
    dma("sp", ods(), bufso[:, 0:7, :], bufs[:, 8:15, :], [], [])
    memset("dve", uext[:, 0:112], 0.0, [b("uext0")])
    ublocks = [(hp7, 0, 16, [b("hp7")]), (hT_own, 0, 512, [hbuf(t) for t in range(8, 12)]),
               (hT_own, 512, 512, [hbuf(t) for t in range(12, 16)]), (hT_own, 1024, 128, [hbuf(16)])]
    for g in range(4):
        w = POOL_W[g]
        for cc in range(2):
            cu = g * 2 + cc
            si = su[cu // 4]
            wc = (cu % 4) * 128
            for bi_, (hb_, c0, n, rd) in enumerate(ublocks):
                pi = nps()
                for k in range(16):
                    mm(PS[pi][:, 0:n], Wv[si][:, k, wc:wc + 128], hb_[:, k, c0:c0 + n], k == 0, k == 15, rd + [WB[si]], [PSB[pi]])
                if bi_ == 0:
                    ts("dve", uext[:, 112:128], PS[pi][:, 0:16], flag, None, ALU.mult, None, [PSB[pi], b("flag")], [b("uext")])
                elif bi_ < 3:
                    cp("act", uext[:, 128 + c0:128 + c0 + n], PS[pi][:, 0:n], [PSB[pi]], [b("uext")], accum=True)
                else:
                    cp("act", usc, PS[pi][:, 0:128], [PSB[pi]], [b("usc")])
            cp("dve", usx[:, :, 15:23], usc.rearrange("p (j t) -> p j t", t=8), [b("usc")], [b("usx")])
            pi = nps()
            for hf_ in range(2):
                tr(PS[pi][:, hf_ * 120:(hf_ + 1) * 120], br[hf_][0:120, cu * 128:(cu + 1) * 128], ident_f[0:120, 0:120],
                   [b("br%d" % hf_), b("K")], [PSB[pi]])
            cp("dve", usx[:, :, 0:15], PS[pi][:, 0:240].rearrange("p (j r) -> p j r", r=15), [PSB[pi]], [b("usx")], accum=True)
            pi = nps()
            tr(PS[pi][:, 0:128], uext[:, 1024:1152], ident_f, [b("uext"), b("K")], [PSB[pi]])
            tr(PS[pi][:, 128:256], usc, ident_f, [b("usc"), b("K")], [PSB[pi]])
            cp("act", utok[:, cu * 128:(cu + 1) * 128], PS[pi][:, 0:128], [PSB[pi]], [b("utok")], accum=True)
            cp("act", utoks[:, cu * 128:(cu + 1) * 128], PS[pi][:, 128:256], [PSB[pi]], [b("utoks")], accum=True)
            cur, curs = uext, usx
            bufsP = [pa_, pb2]; bufsS = [psa, psb2]
            L = 0; sh = 1; it = 0
            while sh < w:
                L += sh
                nx, nxs = bufsP[it % 2], bufsS[it % 2]
                tt("dve", nx[:, L:1152], cur[:, L:1152], cur[:, L - sh:1152 - sh], ALU.add, [b("uext"), b("uext0"), b("pp")], [b("pp")])
                tt("dve", nxs[:, :, L:23], curs[:, :, L:23], curs[:, :, L - sh:23 - sh], ALU.add, [b("usx"), b("pps")], [b("pps")])
                cur, curs = nx, nxs
                sh *= 2; it += 1
            stt(pooled[:, cc, 0:1024], cur[:, 128:1152], 1.0 / w, uext[:, 128:1152], ALU.mult, ALU.subtract,
                [b("pp"), b("uext")], [b("pooled")], accum=(cc == 1))
            t16 = sml[:, 16:32]
            tt("dve", t16, cur[:, 128:144], invtab[:, g * 16:(g + 1) * 16], ALU.mult, [b("pp"), b("K")], [b("t16")])
            tt("dve", pooled[:, cc, 0:16], t16, uext[:, 128:144], ALU.subtract, [b("t16"), b("uext")], [b("pooled")], accum=True)
            stt(pooled[:, cc, 1024:1152].rearrange("p (j t) -> p j t", t=8), curs[:, :, 15:23], 1.0 / w, usx[:, :, 15:23],
                ALU.mult, ALU.subtract, [b("pps"), b("usx")], [b("pooled")], accum=True)
        for dd in range(2):
            for tb in range(3):
                pi = nps()
                for cc in range(2):
                    mm(PS[pi][:, 0:384], wp[:, g, cc, dd * 128:(dd + 1) * 128], pooled[:, cc, tb * 384:(tb + 1) * 384],
                       cc == 0, cc == 1, [b("wp"), b("pooled")], [PSB[pi]])
                act(catT[:, 8 + g * 2 + dd, tb * 384:(tb + 1) * 384], PS[pi][:, 0:384], AF.Copy, [PSB[pi], b("vecT")],
                    [b("catT")], accum=True, scale=pool_scaleT[:, g * 2 + dd:g * 2 + dd + 1])
    dma("sp", ods(), bufp, utok[113:128, :], [b("utok")], [])
    for j in range(16):
        dma("sp", ods(), bufso[j, 7:15, :], utoks[j * 8:(j + 1) * 8, :], [b("utoks")], [])
    dump("uext", uext, [b("uext")])
    dump("pa", pa_, [b("pp")])
    dump("pb", pb2, [b("pp")])
    dump("pooled", V(O_POOLED, [1152], F32), [b("pooled")])
    dump("catT", V(O_CAT, [9216], F32), [b("catT")])
    dump("oTs", V(O_OTS, [1152], F32), [b("oTs")])
    dump("qT", V(O_QT, [1152], F32), [b("qT")])
    dump("kT", V(O_KT, [1152], F32), [b("kT")])
    p.barrier()

    chk(7)
    O_W3 = 87040; O_GP = 103424; O_GS = 111616; O_TMP = 193536; O_EX = 195584
    Wout = [Wv[0], Wv[1], Wv[2], V(O_W3, [16, 512])]
    WoB = [WB[0], WB[1], WB[2], b("W3")]
    wod = [wds[0], wds[1], wds[2], p.dsem("w3")]
    for cb in range(4):
        dma("pool", wod[cb], Wout[cb], w_out[:, cb * 512:(cb + 1) * 512].rearrange("(k p) n -> p k n", p=128), [], [WoB[cb]])
    wrot[0] = 0
    h2T = V(O_HTO, [16, 1152])
    tmpf = V(O_TMP, [512], F32)
    ex = [V(O_EX + 512 * i, [128], F32) for i in range(2)]

    def gt_cols(chunk0, nchunk, which, dst, dstbuf):
        pi = nps()
        for i in range(nchunk):
            e_ = i % 2
            if which == 0:
                cp("dve", ex[e_], adaT[:, chunk0 + i, 0:1].to_broadcast([128, 128]), [b("ada2"), b("ada5")], [b("ex%d" % e_)])
            else:
                cp("dve", ex[e_].rearrange("p (j t) -> p j t", t=8),
                   adaT[:, chunk0 + i, 1:17].unsqueeze(2).to_broadcast([128, 16, 8]), [b("ada2"), b("ada5")], [b("ex%d" % e_)])
            tr(PS[pi][:, (i % 4) * 128:(i % 4 + 1) * 128], ex[e_], ident_f, [b("ex%d" % e_), b("K")], [PSB[pi]])
            if i % 4 == 3 or i == nchunk - 1:
                n0 = (i // 4) * 4
                cp("act", dst[:, n0 * 128:(i + 1) * 128], PS[pi][:, 0:(i - n0 + 1) * 128], [PSB[pi]], [dstbuf], accum=True)
                if i != nchunk - 1:
                    pi = nps()

    gtp = V(O_GP, [2048], F32); gts = V(O_GS, [2048], F32)
    gt_cols(32, 16, 0, gtp, b("gtp"))
    gt_cols(32, 16, 1, gts, b("gts"))
    def p6_mm(t):
        s_ = t % 2
        src = xo[t * 128:(t + 1) * 128, :] if t < 8 else xs
        dma("sp", dxs[s_], xt[s_], src, [], [b("xt%d" % s_)])
        gtb, gtbuf = (gtp, b("gtp")) if t < 8 else (gts, b("gts"))
        for cb in range(4):
            pi = nps()
            for k in range(16):
                mm(PS[pi][:, :], catT[:, k, t * 128:(t + 1) * 128], Wout[cb][:, k, :], k == 0, k == 15,
                   [b("catT"), WoB[cb]], [PSB[pi]])
            tt("dve", tmpf, PS[pi][:, :], gtb[:, cb * 512:(cb + 1) * 512], ALU.mult, [PSB[pi], gtbuf], [b("tmpf")])
            tt("dve", xt[s_][:, cb * 512:(cb + 1) * 512], tmpf, xt[s_][:, cb * 512:(cb + 1) * 512], ALU.add,
               [b("tmpf"), b("xt%d" % s_)], [b("xt%d" % s_)])

    p6_mm(0)
    for t in range(9):
        s = t % 2
        if t + 1 < 9:
            p6_mm(t + 1)
        norm_to_T(t, b("xt%d" % s), xt[s], h2T, t * 128, scale2T, 48, 0, t == 8, b("sc2"), b("ada3"), b("h2T"),
                  O_XT[1])
        dma("pool", dxs_st[s], x1s[t * 128:(t + 1) * 128, :], xt[s], [b("xt%d" % s)], [b("x1s%d" % t)])
    p.barrier()

    chk(8)
    O_ACTA = 50176; O_ACTB = 156672; O_SA = 188928

    def actT(j):
        if j < 30:
            return V(O_ACTA + 2304 * j, [1152])
        return V(O_ACTB + 2304 * (j - 30), [1152])

    sa_t = [V(O_SA + 1536 * i, [384], F32) for i in range(2)]
    fslots = [0, 1]
    fr = [0]
    for s in range(22):
        si = fslots[fr[0]]; fr[0] = (fr[0] + 1) % 2
        j0 = 2 * s
        wload(Wv[si][:, :, 0:256], w_ffn_in[:, j0 * 128:j0 * 128 + 256], si)
        wload(Wv[si][:, :, 256:512], w_ffn_in[:, 5632 + j0 * 128:5632 + j0 * 128 + 256], si, accum=True)
        for jj in range(2):
            j = j0 + jj
            for tb in range(3):
                pa = nps(); pb_ = nps()
                for k in range(16):
                    mm(PS[pa][:, 0:384], Wv[si][:, k, jj * 128:(jj + 1) * 128], h2T[:, k, tb * 384:(tb + 1) * 384],
                       k == 0, k == 15, [b("h2T"), WB[si]], [PSB[pa]])
                for k in range(16):
                    mm(PS[pb_][:, 0:384], Wv[si][:, k, 256 + jj * 128:256 + (jj + 1) * 128], h2T[:, k, tb * 384:(tb + 1) * 384],
                       k == 0, k == 15, [b("h2T"), WB[si]], [PSB[pb_]])
                e_ = tb % 2
                act(sa_t[e_], PS[pa][:, 0:384], AF.Silu, [PSB[pa]], [b("sa%d" % e_)])
                tt("dve", actT(j)[:, tb * 384:(tb + 1) * 384], sa_t[e_], PS[pb_][:, 0:384], ALU.mult,
                   [b("sa%d" % e_), PSB[pb_]], [b("actT")], accum=True)
    p.barrier()

    chk(9)
    O_SLA = 17408; O_SLB = 119808; O_XP = 142336; O_G2 = 146432; O_TM2 = 150528
    fsl = [V(O_SLA, [44, 256]), V(O_SLB, [44, 256])]
    fsb = [b("FA"), b("FB")]
    fds = [p.dsem("fa"), p.dsem("fb")]
    xp = [V(O_XP + 1024 * i, [256], F32) for i in range(4)]
    g2 = [[V(O_G2 + 2048 * i + 1024 * w_, [256], F32) for w_ in range(2)] for i in range(2)]
    tm2 = V(O_TM2, [256], F32)
    dxp = [p.dsem("xp%d" % i) for i in range(4)]
    dxp_st = [p.dsem("xps%d" % i) for i in range(4)]
    xr = [0]
    def fload_w(cb):
        si = cb % 2
        dma("pool", fds[si], fsl[si], w_ffn_out[:, cb * 256:(cb + 1) * 256].rearrange("(c p) n -> p c n", p=128), [], [fsb[si]])

    fload_w(0)
    for cb in range(8):
        si = cb % 2
        if cb + 1 < 8:
            fload_w(cb + 1)
        gt_cols(80 + 2 * cb, 2, 0, g2[si][0], b("g2p%d" % si))
        gt_cols(80 + 2 * cb, 2, 1, g2[si][1], b("g2s%d" % si))
        for t in range(9):
            r = xr[0]; xr[0] = (r + 1) % 4
            dma("sp", dxp[r], xp[r], x1s[t * 128:(t + 1) * 128, cb * 256:(cb + 1) * 256], [b("x1s%d" % t)], [b("xp%d" % r)])
            pi = nps()
            for j in range(44):
                mm(PS[pi][:, 0:256], actT(j)[:, t * 128:(t + 1) * 128], fsl[si][:, j, :], j == 0, j == 43,
                   [b("actT"), fsb[si]], [PSB[pi]])
            w_ = 0 if t < 8 else 1
            tt("dve", tm2, PS[pi][:, 0:256], g2[si][w_], ALU.mult, [PSB[pi], b("g2p%d" % si), b("g2s%d" % si)], [b("tm2")])
            tt("dve", xp[r], tm2, xp[r], ALU.add, [b("tm2"), b("xp%d" % r)], [b("xp%d" % r)])
            dma("pool", dxp_st[r], x2s[t * 128:(t + 1) * 128, cb * 256:(cb + 1) * 256], xp[r], [b("xp%d" % r)], [b("x2s%d" % t)], accum=True)
    p.barrier()

    chk(10)
    O_XF = [50176, 58368, 66560]; O_GF = 74752; O_JK = 82944
    xf = [V(o, [2048], F32) for o in O_XF]
    gfin = V(O_GF, [2048], F32); junk = V(O_JK, [2048])
    dma("sp", one(), gfin, g_final.partition_broadcast(128), [], [b("gfin")])
    dxf = [p.dsem("xf%d" % i) for i in range(3)]
    dxf_st = [p.dsem("xfs%d" % i) for i in range(3)]
    def fload(t):
        r = t % 3
        dma("sp", dxf[r], xf[r], x2s[t * 128:(t + 1) * 128, :], [b("x2s%d" % t)], [b("xf%d" % r)])

    for t in range(3):
        fload(t)
    for t in range(9):
        r = t % 3
        s0 = ssq[:, (t % 2) * 4 + 0:(t % 2) * 4 + 1]
        s1 = ssq[:, (t % 2) * 4 + 1:(t % 2) * 4 + 2]
        s2 = ssq[:, (t % 2) * 4 + 2:(t % 2) * 4 + 3]
        sb_ = b("ssq%d" % (t % 2))
        act(junk, xf[r], AF.Square, [b("xf%d" % r)], [b("junk"), sb_], accum_out=s0)
        act(s1, s0, AF.Sqrt, [sb_, b("eps")], [sb_], accum=True, scale=1.0 / 2048.0, bias=epsc)
        recip(s2, s1, [sb_], [sb_], accum=True)
        stt(xf[r], xf[r], s2, gfin, ALU.mult, ALU.mult, [b("xf%d" % r), sb_, b("gfin")], [b("xf%d" % r)])
        dst = yo[t * 128:(t + 1) * 128, :] if t < 8 else ys
        dma("pool", dxf_st[r], dst, xf[r], [b("xf%d" % r)], [])
        if t + 3 < 9:
            fload(t + 3)

    p.emit()
    st.close()
    return nc


def _consts(hf):
    K = np.zeros((128, NK), np.float32)
    s = np.arange(128)
    K[:, 0:128] = np.eye(128, dtype=np.float32)
    K[:, 128:256] = (s[:, None] <= s[None, :]).astype(np.float32)
    K[:, 256:384] = ((s[:, None] // 8 == s[None, :] // 8) & (s[:, None] <= s[None, :])).astype(np.float32)
    K[:, 384:400] = (s[:, None] // 8 == np.arange(16)[None, :]).astype(np.float32)
    K[0:4, 400:404] = np.eye(4, dtype=np.float32)
    K[:, 404:420] = (s[:, None] == (8 * np.arange(16) + 7)[None, :]).astype(np.float32)
    for g, w in enumerate(POOL_W):
        for t in range(16):
            cnt = min(t + 1, w) if hf == 0 else w
            K[:, 420 + g * 16 + t] = 1.0 / cnt
    K[0:4, 484:612] = (s % 8 != 0).astype(np.float32)[None, :]
    K[0:4, 612:740] = np.where(s % 8 == 0, -1e30, 0.0).astype(np.float32)[None, :]
    K[0:4, 740:868] = 1.0
    return K


_NC = [None]


def kernel(x_prompt, x_sample, state_mlstm_C, state_mlstm_n, state_mlstm_m, state_pool_buf,
           c_prompt, c_sample, w_ada, b_ada, g_mix, w_in, b_gate, g_head, w_pool, pool_scale,
           w_out, g_ffn, w_ffn_in, w_ffn_out, g_final):
    f = lambda a: np.ascontiguousarray(np.asarray(a, dtype=np.float32))
    x_prompt = f(x_prompt); x_sample = f(x_sample)
    Cst_ = f(state_mlstm_C)[0]; nst = f(state_mlstm_n)[0]; mst = f(state_mlstm_m)[0]; bst = f(state_pool_buf)[0]
    c_prompt = f(c_prompt); c_sample = f(c_sample)
    shared = {"w_ada": f(w_ada)[0], "b_ada": f(b_ada)[0], "g_mix": f(g_mix)[0], "w_in": f(w_in)[0],
              "b_gate": f(b_gate)[0], "g_head": f(g_head)[0], "w_pool": f(w_pool)[0], "pool_scale": f(pool_scale)[0],
              "w_out": f(w_out)[0], "g_ffn": f(g_ffn)[0], "w_ffn_in": f(w_ffn_in)[0], "w_ffn_out": f(w_ffn_out)[0],
              "g_final": f(g_final)}
    in_maps = []
    for c in range(8):
        bb, hf = c // 2, c % 2
        sl = slice(16 * c, 16 * c + 16)
        m = dict(shared)
        m["xo"] = np.ascontiguousarray(x_prompt[bb, hf * 1024:(hf + 1) * 1024])
        m["xpre"] = np.ascontiguousarray(x_prompt[bb, 0:1024])
        m["xs"] = np.ascontiguousarray(x_sample[sl].reshape(128, 2048))
        m["cst"] = np.ascontiguousarray(np.concatenate([c_prompt[bb:bb + 1], c_sample[sl]], axis=0))
        m["Cs"] = np.ascontiguousarray(Cst_[sl].reshape(64, 256, 256))
        m["ns"] = np.ascontiguousarray(nst[sl].reshape(64, 256))
        m["ms"] = np.ascontiguousarray(mst[sl])
        m["bufs"] = np.ascontiguousarray(bst[sl])
        m["flag"] = np.full((128, 1), float(hf), np.float32)
        m["K"] = _consts(hf)
        in_maps.append(m)
    if _NC[0] is None:
        _NC[0] = build()
    res = run_bass_kernel_spmd(_NC[0], in_maps, core_ids=list(range(8)))
    R = res.results
    y_p = np.zeros((4, 2048, 2048), np.float32)
    for c in range(8):
        y_p[c // 2, (c % 2) * 1024:(c % 2 + 1) * 1024] = R[c]["yo"]
    y_s = np.concatenate([R[c]["ys"].reshape(16, 8, 2048) for c in range(8)], axis=0)
    C_p = np.stack([R[2 * i + 1]["Cp"] for i in range(4)])[None]
    n_p = np.stack([R[2 * i + 1]["npo"] for i in range(4)])[None]
    m_p = np.stack([R[2 * i + 1]["mpo"].reshape(4) for i in range(4)])[None]
    b_p = np.stack([R[2 * i + 1]["bufp"] for i in range(4)])[None]
    C_s = np.concatenate([R[c]["Cso"].reshape(16, 4, 256, 256) for c in range(8)], axis=0)[None]
    n_s = np.concatenate([R[c]["nso"].reshape(16, 4, 256) for c in range(8)], axis=0)[None]
    m_s = np.concatenate([R[c]["mso"] for c in range(8)], axis=0)[None]
    b_s = np.concatenate([R[c]["bufso"] for c in range(8)], axis=0)[None]
    return (y_p, y_s, C_p, n_p, m_p, b_p, C_s, n_s, m_s, b_s)
```

```python
import contextlib
import numpy as np
import concourse.bass as bass
import concourse.mybir as mybir
from concourse.bass_utils import run_bass_kernel_spmd

F32 = mybir.dt.float32
BF16 = mybir.dt.bfloat16
AF = mybir.ActivationFunctionType
ALU = mybir.AluOpType

EPS = 1e-6
NK = 868
POOL_W = (2, 4, 8, 16)


class Buf:
    __slots__ = ("name", "writers", "readers", "excl")

    def __init__(self, name="", excl=False):
        self.name = name
        self.writers = {}
        self.readers = {}
        self.excl = excl


class DSem:
    __slots__ = ("name", "n", "sem")

    def __init__(self, name):
        self.name = name
        self.n = 0
        self.sem = None


class Op:
    __slots__ = ("fn", "waits", "sig", "dsem", "rank")

    def __init__(self, fn, waits, dsem):
        self.fn = fn
        self.waits = waits
        self.sig = False
        self.dsem = dsem
        self.rank = 0


class Prog:
    ENG = ("pe", "act", "dve", "pool", "sp")

    def __init__(self, nc):
        self.nc = nc
        self.ops = {e: [] for e in self.ENG}
        self.seen = {e: {} for e in self.ENG}
        self.last = {e: -1 for e in self.ENG}
        self.pending = {e: {} for e in self.ENG}
        self.dsems = []
        self.off = False

    def dsem(self, name):
        d = DSem(name)
        self.dsems.append(d)
        return d

    def barrier(self):
        if self.off:
            return
        ev = {}
        for e in self.ENG:
            if self.last[e] >= 0:
                ev[e] = self.last[e]
        for d in self.dsems:
            if d.n > 0:
                ev[d] = d.n
        for e in self.ENG:
            pe = self.pending[e]
            for s, n in ev.items():
                if pe.get(s, -1) < n:
                    pe[s] = n

    def _deps(self, eng, reads, writes, accum, own=None):
        own = own if own is not None else eng
        deps = dict(self.pending[eng])
        self.pending[eng] = {}

        def add(ev):
            for s, n in ev.items():
                if deps.get(s, -1) < n:
                    deps[s] = n

        for b in reads:
            add(b.writers)
            if b.excl:
                add({s_: n_ for s_, n_ in b.readers.items() if s_ != eng})
        for b in writes:
            add(b.readers)
            if not accum:
                add(b.writers)
            else:
                add({s_: n_ for s_, n_ in b.writers.items() if s_ != own or (own == eng and eng != "pe")})
        waits = []
        seen = self.seen[eng]
        for s, n in deps.items():
            if s == "pe" and eng == "pe":
                continue
            if seen.get(s, -1) >= n:
                continue
            seen[s] = n
            waits.append((s, n))
        return waits

    def op(self, eng, fn, reads=(), writes=(), accum=False):
        if self.off:
            return
        waits = self._deps(eng, reads, writes, accum)
        lst = self.ops[eng]
        idx = len(lst)
        lst.append(Op(fn, waits, None))
        self.last[eng] = idx
        for b in reads:
            if b.readers.get(eng, -1) < idx:
                b.readers[eng] = idx
        for b in writes:
            if accum:
                b.writers[eng] = idx
            else:
                b.writers = {eng: idx}
                b.readers = {}
        return idx

    def dma(self, eng, dsem, fn, reads=(), writes=(), accum=False):
        if self.off:
            return
        waits = self._deps(eng, reads, writes, accum, own=dsem)
        dsem.n += 1
        n = dsem.n
        self.ops[eng].append(Op(fn, waits, dsem))
        for b in reads:
            if b.readers.get(dsem, -1) < n:
                b.readers[dsem] = n
        for b in writes:
            if accum:
                b.writers[dsem] = n
            else:
                b.writers = {dsem: n}
                b.readers = {}

    def emit(self, final_eng="sp"):
        nc = self.nc
        fin = [(d, d.n) for d in self.dsems if d.n > 0]
        for e in self.ENG:
            for o in self.ops[e]:
                for s, n in o.waits:
                    if isinstance(s, str):
                        self.ops[s][n].sig = True
        for e in self.ENG:
            r = 0
            for o in self.ops[e]:
                if o.sig:
                    r += 1
                    o.rank = r
        with contextlib.ExitStack() as st:
            esem = {e: st.enter_context(nc.semaphore("s_" + e)) for e in self.ENG}
            for i, d in enumerate(self.dsems):
                if d.n > 0:
                    d.sem = st.enter_context(nc.semaphore("d%d_%s" % (i, d.name)))
            block = st.enter_context(nc.Block())
            regs = {"pe": block.tensor, "act": block.scalar, "dve": block.vector,
                    "pool": block.gpsimd, "sp": block.sync}

            def make(e):
                def body(eng):
                    for o in self.ops[e]:
                        for s, n in o.waits:
                            if isinstance(s, str):
                                eng.wait_ge(esem[s], self.ops[s][n].rank)
                            else:
                                eng.wait_ge(s.sem, 16 * n)
                        ins = o.fn(eng)
                        if o.dsem is not None:
                            ins.then_inc(o.dsem.sem, 16)
                        elif o.sig:
                            ins.then_inc(esem[e], 1)
                    if e == final_eng:
                        for d, n in fin:
                            eng.wait_ge(d.sem, 16 * n)
                return body

            for e in self.ENG:
                regs[e](make(e))


def build():
    nc = bass.Bass("TRN2", target_bir_lowering=False)

    def din(name, shape):
        return nc.dram_tensor(name, list(shape), F32, kind="ExternalInput").ap()

    def dout(name, shape):
        return nc.dram_tensor(name, list(shape), F32, kind="ExternalOutput").ap()

    xo = din("xo", [1024, 2048]); xpre = din("xpre", [1024, 2048]); xs = din("xs", [128, 2048])
    cst = din("cst", [17, 2048]); Cs = din("Cs", [64, 256, 256]); ns = din("ns", [64, 256])
    ms = din("ms", [16, 4]); bufs = din("bufs", [16, 15, 1024]); flagd = din("flag", [128, 1])
    Kd = din("K", [128, NK])
    w_ada = din("w_ada", [2048, 12288]); b_ada = din("b_ada", [12288]); g_mix = din("g_mix", [2048])
    w_in = din("w_in", [2048, 5128]); b_gate = din("b_gate", [8]); g_head = din("g_head", [1024])
    w_pool = din("w_pool", [4, 256, 256]); pool_scale = din("pool_scale", [1024])
    w_out = din("w_out", [2048, 2048]); g_ffn = din("g_ffn", [2048])
    w_ffn_in = din("w_ffn_in", [2048, 11264]); w_ffn_out = din("w_ffn_out", [5632, 2048])
    g_final = din("g_final", [2048])
    yo = dout("yo", [1024, 2048]); ys = dout("ys", [128, 2048])
    Cp = dout("Cp", [4, 256, 256]); npo = dout("npo", [4, 256]); mpo = dout("mpo", [4, 1])
    bufp = dout("bufp", [15, 1024])
    Cso = dout("Cso", [64, 256, 256]); nso = dout("nso", [64, 256]); mso = dout("mso", [16, 4])
    bufso = dout("bufso", [16, 15, 1024])
    DBG = False
    dbg = dout("dbg", [128, 24576]) if DBG else None
    dbgpos = [0]
    dbgmap = {}

    def dump(name, ap, rdbufs, parts=128):
        if not DBG or p.off:
            return
        n = ap.shape[1]
        c0 = dbgpos[0]
        dbgpos[0] += n
        dbgmap[name] = (c0, n, parts)
        dma("sp", one(), dbg[0:parts, c0:c0 + n], ap, rdbufs, [])

    x1s = nc.dram_tensor("x1s", [1152, 2048], F32, kind="Internal").ap()
    x2s = nc.dram_tensor("x2s", [1152, 2048], F32, kind="Internal").ap()

    st = contextlib.ExitStack()
    arena = st.enter_context(nc.sbuf_tensor("arena", [128, 105472], BF16))
    PS = [st.enter_context(nc.psum_tensor("ps%d" % i, [128, 512], F32)) for i in range(8)]
    PSB = [Buf("ps%d" % i, excl=True) for i in range(8)]
    p = Prog(nc)
    STOP = 99

    def chk(k):
        if k > STOP:
            p.off = True

    def V(off, shape, dt=BF16, parts=128):
        shape = list(shape)
        n = int(np.prod(shape))
        assert off % 4 == 0
        if dt == BF16:
            assert off + 2 * n <= 210944, (off, shape)
            ap = arena[0:parts, off // 2: off // 2 + n]
        else:
            assert off + 4 * n <= 210944, (off, shape)
            ap = arena[0:parts, off // 2: off // 2 + 2 * n].bitcast(F32)
        if len(shape) == 2:
            ap = ap.rearrange("p (a b) -> p a b", b=shape[1])
        elif len(shape) == 3:
            ap = ap.rearrange("p (a b c) -> p a b c", b=shape[1], c=shape[2])
        return ap

    def psb(i):
        return PS[i][:, :].bitcast(BF16)

    rot = [0]
    rotn = [6]

    def nps():
        i = rot[0]
        rot[0] = (i + 1) % rotn[0]
        return i

    def mm(out, lhsT, rhs, start, stop, rd, wr):
        p.op("pe", lambda e: e.matmul(out, lhsT=lhsT, rhs=rhs, start=start, stop=stop), reads=rd, writes=wr)

    def tr(out, in_, ident, rd, wr):
        p.op("pe", lambda e: e.transpose(out=out, in_=in_, identity=ident), reads=rd, writes=wr)

    def act(out, in_, func, rd, wr, accum=False, **kw):
        p.op("act", lambda e: e.activation(out=out, in_=in_, func=func, **kw), reads=rd, writes=wr, accum=accum)

    def tt(eng, out, in0, in1, op, rd, wr, accum=False):
        p.op(eng, lambda e: e.tensor_tensor(out=out, in0=in0, in1=in1, op=op), reads=rd, writes=wr, accum=accum)

    def ts(eng, out, in0, s1, s2, op0, op1, rd, wr, accum=False):
        if s2 is None:
            p.op(eng, lambda e: e.tensor_scalar(out=out, in0=in0, scalar1=s1, scalar2=None, op0=op0), reads=rd, writes=wr, accum=accum)
        else:
            p.op(eng, lambda e: e.tensor_scalar(out=out, in0=in0, scalar1=s1, scalar2=s2, op0=op0, op1=op1), reads=rd, writes=wr, accum=accum)

    def stt(out, in0, scalar, in1, op0, op1, rd, wr, accum=False):
        p.op("dve", lambda e: e.scalar_tensor_tensor(out=out, in0=in0, scalar=scalar, in1=in1, op0=op0, op1=op1), reads=rd, writes=wr, accum=accum)

    def cp(eng, out, in_, rd, wr, accum=False):
        if eng == "act":
            p.op("act", lambda e: e.copy(out=out, in_=in_), reads=rd, writes=wr, accum=accum)
        else:
            p.op(eng, lambda e: e.tensor_copy(out=out, in_=in_), reads=rd, writes=wr, accum=accum)

    def memset(eng, ap, val, wr, accum=False):
        p.op(eng, lambda e: e.memset(ap, val), writes=wr, accum=accum)

    def recip(out, in_, rd, wr, accum=False):
        p.op("dve", lambda e: e.reciprocal(out=out, in_=in_), reads=rd, writes=wr, accum=accum)

    def scan(out, d0, d1, init, op0, op1, rd, wr):
        p.op("dve", lambda e: e.tensor_tensor_scan(out=out, data0=d0, data1=d1, initial=init, op0=op0, op1=op1), reads=rd, writes=wr)

    def dma(q, ds, out, in_, rd, wr, accum=False, nc_ok=False):
        if nc_ok:
            p.dma(q, ds, lambda e: e.dma_start(out=out, in_=in_, allow_slow_non_contiguous=True), reads=rd, writes=wr, accum=accum)
        else:
            p.dma(q, ds, lambda e: e.dma_start(out=out, in_=in_), reads=rd, writes=wr, accum=accum)

    O_K = 0
    O_IDB = 3584; O_MCB = 3840; O_MBB = 4096
    O_ADA = 4352
    O_SC1 = 10880; O_SC2 = 11968
    O_VEC = 13056
    O_SCT = 13632
    O_TOK = 14176
    O_DBP = 15264; O_DBO = 15392; O_DBS = 15520; O_DSJ = 15776
    O_MISC = 15808
    O_WGB = 16448; O_WGF = 16704
    O_W = [17408, 33792, 50176]
    O_XT = [66560, 74752]; O_XN = 82944
    P0 = 87040
    O_HTP = 87040; O_HTO = 119808
    O_ROW = 156672

    Ksb = V(O_K, [NK], F32)
    ident_f = Ksb[:, 0:128]
    BD16 = Ksb[:, 384:400]; E4 = Ksb[0:4, 400:404]; sellast = Ksb[:, 404:420]
    invtab = Ksb[:, 420:484]; segmul = Ksb[0:4, 484:612]; segadd = Ksb[0:4, 612:740]
    ones4 = Ksb[0:4, 740:868]
    ident_b = V(O_IDB, [128]); maskC_b = V(O_MCB, [128]); maskBD_b = V(O_MBB, [128])
    adaT = V(O_ADA, [96, 17], F32)
    scale1T = V(O_SC1, [16, 17], F32); scale2T = V(O_SC2, [16, 17], F32)
    vecT = V(O_VEC, [144], F32)
    b_adaT = vecT[:, 0:96]; g_mixT = vecT[:, 96:112]; g_ffnT = vecT[:, 112:128]
    g_headT = vecT[:, 128:136]; pool_scaleT = vecT[:, 136:144]
    scT = V(O_SCT, [16, 17])
    tok = V(O_TOK, [17, 16], F32)
    dbp = V(O_DBP, [32], F32); dbo = V(O_DBO, [32], F32); dbs = V(O_DBS, [64], F32)
    dsj = V(O_DSJ, [4], F32, parts=16)
    misc = V(O_MISC, [160], F32)
    wg_b = V(O_WGB, [16, 8]); wg_f = V(O_WGF, [16, 8], F32)
    Wv = [V(o, [16, 512]) for o in O_W]
    xt = [V(o, [2048], F32) for o in O_XT]
    xn = V(O_XN, [2048])
    xn_main = xn
    hT_pre = V(O_HTP, [16, 1024]); hT_own = V(O_HTO, [16, 1152])

    flag = misc[:, 0:1]; epsc = misc[:, 1:2]
    bi = misc[0:4, 2:3]; bfn = misc[0:4, 3:4]; nbf = misc[0:4, 4:5]
    mpre = misc[0:4, 5:6]; m_in = misc[0:4, 6:7]; mfo = misc[0:4, 7:8]
    Gs = misc[0:4, 16:32]; Ge = misc[0:4, 32:48]; dec = misc[0:4, 48:64]; mfs = misc[0:4, 64:80]
    msr = misc[0:4, 80:96]; tmpg = misc[0:4, 96:112]
    ssq = misc[:, 112:120]
    sml = misc[:, 120:160]

    B = {}

    def b(name):
        if name not in B:
            B[name] = Buf(name)
        return B[name]

    WB = [b("W0"), b("W1"), b("W2")]
    wds = [p.dsem("w0"), p.dsem("w1"), p.dsem("w2")]
    wrot = [0]

    def wslot():
        i = wrot[0]
        wrot[0] = (i + 1) % 3
        return i

    onecnt = [0]

    def one():
        onecnt[0] += 1
        return p.dsem("one%d" % onecnt[0])
    dxs = [p.dsem("x0"), p.dsem("x1")]
    dxs_st = [p.dsem("xs0"), p.dsem("xs1")]
    dout_s = [p.dsem("o%d" % i) for i in range(4)]
    orot = [0]

    def ods():
        i = orot[0]
        orot[0] = (i + 1) % 4
        return dout_s[i]

    dma("sp", one(), Ksb, Kd, [], [b("K")])
    dma("sp", one(), flag, flagd, [], [b("flag")])
    cs_f = V(O_XT[0], [2048], F32, parts=17)
    cs_b = V(O_XN, [2048], BF16, parts=17)
    dma("sp", one(), cs_f, cst, [], [b("xt0")])
    vr1 = V(O_XT[1], [128], F32, parts=96)
    vr2 = V(O_XT[1] + 512, [128], F32, parts=48)
    dma("sp", one(), vr1, b_ada.rearrange("(c p) -> c p", p=128), [], [b("xt1")])
    dma("sp", one(), vr2[0:16, :], g_mix.rearrange("(c p) -> c p", p=128), [], [b("xt1")], accum=True)
    dma("sp", one(), vr2[16:32, :], g_ffn.rearrange("(c p) -> c p", p=128), [], [b("xt1")], accum=True)
    dma("sp", one(), vr2[32:40, :], g_head.rearrange("(c p) -> c p", p=128), [], [b("xt1")], accum=True)
    dma("sp", one(), vr2[40:48, :], pool_scale.rearrange("(c p) -> c p", p=128), [], [b("xt1")], accum=True)
    dma("sp", one(), wg_f, w_in[:, 4096:4104].rearrange("(k p) n -> p k n", p=128), [], [b("wgf")], nc_ok=True)
    dma("sp", one(), bi, b_gate[0:4].rearrange("(p o) -> p o", o=1), [], [b("bi")], nc_ok=True)
    dma("sp", one(), bfn, b_gate[4:8].rearrange("(p o) -> p o", o=1), [], [b("bfn")], nc_ok=True)
    dma("sp", one(), msr, ms.rearrange("j h -> h j"), [], [b("msr")], nc_ok=True)

    cp("dve", ident_b, ident_f, [b("K")], [b("idb")])
    cp("dve", maskC_b, Ksb[:, 128:256], [b("K")], [b("mcb")])
    cp("dve", maskBD_b, Ksb[:, 256:384], [b("K")], [b("mbb")])
    memset("dve", epsc, EPS, [b("eps")])
    cp("dve", wg_b, wg_f, [b("wgf")], [b("wgb")])
    ts("dve", nbf, bfn, -1.0, None, ALU.mult, None, [b("bfn")], [b("nbf")])

    act(cs_b, cs_f, AF.Silu, [b("xt0")], [b("xn")])
    i0 = 6
    for k in range(16):
        tr(psb(i0)[0:128, k * 32: k * 32 + 17], cs_b[0:17, k * 128:(k + 1) * 128], ident_b[0:17, 0:17],
           [b("xn"), b("idb")], [PSB[i0]])
    cp("dve", scT, psb(i0)[:, 0:512].rearrange("p (k c) -> p k c", c=32)[:, :, 0:17], [PSB[i0]], [b("scT")])
    i1 = 7
    tr(PS[i1][:, 0:96], vr1[0:96, :], ident_f[0:96, 0:96], [b("xt1"), b("K")], [PSB[i1]])
    tr(PS[i1][:, 96:144], vr2[0:48, :], ident_f[0:48, 0:48], [b("xt1"), b("K")], [PSB[i1]])
    cp("dve", vecT, PS[i1][:, 0:144], [PSB[i1]], [b("vecT")])

    chk(1)
    def wload(dst, src_cols_ap, si, accum=False):
        dma("pool", wds[si], dst, src_cols_ap.rearrange("(k p) n -> p k n", p=128), [], [WB[si]], accum=accum)

    def ada_load(bi_, si):
        wload(Wv[si], w_ada[:, bi_ * 512:(bi_ + 1) * 512], si)

    def ada_mm(bi_, si):
        pi = nps()
        for cc in range(4):
            for k in range(16):
                mm(PS[pi][:, cc * 32: cc * 32 + 17], Wv[si][:, k, cc * 128:(cc + 1) * 128], scT[:, k, :],
                   k == 0, k == 15, [WB[si], b("scT")], [PSB[pi]])
        tt("dve", adaT[:, bi_ * 4:(bi_ + 1) * 4, :],
           PS[pi][:, 0:128].rearrange("p (c n) -> p c n", n=32)[:, :, 0:17],
           b_adaT[:, bi_ * 4:(bi_ + 1) * 4].unsqueeze(2).to_broadcast([128, 4, 17]), ALU.add,
           [PSB[pi], b("vecT")], [b("ada%d" % (bi_ // 4))], accum=True)

    def ada_block(bi_):
        si = wslot()
        ada_load(bi_, si)
        ada_mm(bi_, si)

    for bi_ in (4, 5, 6, 7, 0, 1, 2, 3):
        ada_block(bi_)
    stt(scale1T, adaT[:, 16:32, :], 1.0, g_mixT.unsqueeze(2).to_broadcast([128, 16, 17]), ALU.add, ALU.mult,
        [b("ada1"), b("vecT")], [b("sc1")])
    ada_rest = list(range(8, 24))

    chk(2)
    xn_alt = [None]

    def norm_to_T(ti, src_tile_buf, xtile, dstT, col0, scaleT, shift_c0, seqcol, sample, scb, shb, dstbuf, tmp4k):
        if xn_alt[0] is not None and ti % 2 == 1:
            xn, xnb = xn_alt[0], b("xn2")
        else:
            xn, xnb = xn_main, b("xn")
        s0 = ssq[:, (ti % 2) * 4 + 0:(ti % 2) * 4 + 1]
        s1 = ssq[:, (ti % 2) * 4 + 1:(ti % 2) * 4 + 2]
        s2 = ssq[:, (ti % 2) * 4 + 2:(ti % 2) * 4 + 3]
        sb_ = b("ssq%d" % (ti % 2))
        act(xn, xtile, AF.Square, [src_tile_buf], [xnb, sb_], accum_out=s0)
        act(s1, s0, AF.Sqrt, [sb_, b("eps")], [sb_], accum=True, scale=1.0 / 2048.0, bias=epsc)
        recip(s2, s1, [sb_], [sb_], accum=True)
        act(xn, xtile, AF.Copy, [src_tile_buf, sb_], [xnb], scale=s2)
        pa, pb_ = nps(), nps()
        for k in range(16):
            pi = pa if k < 8 else pb_
            tr(psb(pi)[:, (k % 8) * 128:(k % 8 + 1) * 128], xn[:, k * 128:(k + 1) * 128], ident_b,
               [xnb, b("idb")], [PSB[pi]])
        if not sample:
            for k in range(16):
                pi = pa if k < 8 else pb_
                src = psb(pi)[:, (k % 8) * 128:(k % 8 + 1) * 128]
                dst = dstT[:, k, col0:col0 + 128]
                if k < 8:
                    act(dst, src, AF.Identity, [PSB[pi], scb, shb], [dstbuf], accum=True,
                        scale=scaleT[:, k, seqcol:seqcol + 1], bias=adaT[:, shift_c0 + k, seqcol:seqcol + 1])
                else:
                    ts("dve", dst, src, scaleT[:, k, seqcol:seqcol + 1], adaT[:, shift_c0 + k, seqcol:seqcol + 1],
                       ALU.mult, ALU.add, [PSB[pi], scb, shb], [dstbuf], accum=True)
        else:
            for hb in range(2):
                pi = pa if hb == 0 else pb_
                k0 = hb * 8
                tmp = V(tmp4k, [8, 16, 8], F32)
                tt("dve", tmp, psb(pi)[:, :].rearrange("p (k j t) -> p k j t", j=16, t=8),
                   scaleT[:, k0:k0 + 8, 1:17].unsqueeze(3).to_broadcast([128, 8, 16, 8]), ALU.mult,
                   [PSB[pi], scb], [b("xt1")])
                tt("dve", dstT[:, k0:k0 + 8, col0:col0 + 128].rearrange("p k (j t) -> p k j t", t=8), tmp,
                   adaT[:, shift_c0 + k0:shift_c0 + k0 + 8, 1:17].unsqueeze(3).to_broadcast([128, 8, 16, 8]), ALU.add,
                   [b("xt1"), shb], [dstbuf], accum=True)

    def hbuf(ti):
        return b("hT%d" % ti)

    xn_alt[0] = V(O_ROW + 17408, [2048])
    for ti in range(17):
        if ti < 8:
            src = xpre[ti * 128:(ti + 1) * 128, :]; dstT = hT_pre; col0 = ti * 128
        elif ti < 16:
            src = xo[(ti - 8) * 128:(ti - 7) * 128, :]; dstT = hT_own; col0 = (ti - 8) * 128
        else:
            src = xs; dstT = hT_own; col0 = 1024
        s = ti % 2
        dma("sp", dxs[s], xt[s], src, [], [b("xt%d" % s)])
        norm_to_T(ti, b("xt%d" % s), xt[s], dstT, col0, scale1T, 0, 0, ti == 16, b("sc1"), b("ada0"),
                  hbuf(ti), O_XT[1])
        if ti % 4 == 2:
            ada_block(ada_rest.pop(0))
    xn_alt[0] = None

    chk(3)
    R_IG = O_ROW; R_LF = O_ROW + 8704; R0 = O_ROW + 17408

    def rowt(i, n=1024):
        return V(R0 + 4096 * i, [n], F32, parts=4)

    ig = V(R_IG, [2176], F32, parts=4); lf = V(R_LF, [2176], F32, parts=4)
    Bc, Ev, Gv, T1, Wp, Wpp, Rv, En, ONES = [rowt(i) for i in range(9)]
    memset("dve", ONES, 1.0, [b("ones")])
    blocks = [(hT_pre, 0, 512, 0, range(0, 4)), (hT_pre, 512, 512, 512, range(4, 8)),
              (hT_own, 0, 512, 1024, range(8, 12)), (hT_own, 512, 512, 1536, range(12, 16)),
              (hT_own, 1024, 128, 2048, range(16, 17))]
    for (hb_, c0, n, ro, tis) in blocks:
        rd = [hbuf(t) for t in tis] + [b("wgb")]
        for part in range(2):
            pi = nps()
            for k in range(16):
                mm(PS[pi][0:4, 0:n], wg_b[:, k, part * 4:(part + 1) * 4], hb_[:, k, c0:c0 + n], k == 0, k == 15,
                   rd, [PSB[pi]])
            if part == 0:
                act(ig[:, ro:ro + n], PS[pi][0:4, 0:n], AF.Identity, [PSB[pi], b("bi")], [b("ig")], accum=True, bias=bi)
            else:
                act(lf[:, ro:ro + n], PS[pi][0:4, 0:n], AF.Exp, [PSB[pi], b("nbf")], [b("lf")], accum=True, scale=-1.0, bias=nbf)
    act(lf, lf, AF.Ln, [b("lf")], [b("lf")], bias=1.0)
    ts("dve", lf, lf, -1.0, None, ALU.mult, None, [b("lf")], [b("lf")])

    rB, rE, rG, rT, rWp, rWpp, rR, rEn, rS = [b(n) for n in ("rB", "rE", "rG", "rT", "rWp", "rWpp", "rR", "rEn", "rS")]

    def seg_rows(c0, own):
        lfs = lf[:, c0:c0 + 1024]; igs = ig[:, c0:c0 + 1024]
        v3 = lambda a: a.rearrange("p (c t) -> p c t", t=128)
        scan(Bc, ONES, lfs, 0.0, ALU.mult, ALU.add, [b("ones"), b("lf")], [rB])
        tt("dve", Ev, igs, Bc, ALU.subtract, [rB, b("ig")], [rE])
        if own:
            scan(Gv, ONES, Ev, m_in, ALU.mult, ALU.max, [rE, b("ones"), b("m_in")], [rG])
            cp("dve", Gs[:, 0:1], m_in, [b("m_in")], [rS])
        else:
            scan(Gv, ONES, Ev, 0.0, ALU.mult, ALU.max, [rE, b("ones")], [rG])
            memset("dve", Gs[:, 0:1], 0.0, [rS])
        G3 = v3(Gv)
        cp("dve", Ge[:, 0:8], G3[:, :, 127], [rG], [rS], accum=True)
        cp("dve", Gs[:, 1:8], G3[:, 0:7, 127], [rG], [rS], accum=True)
        Geb = Ge[:, 0:8].unsqueeze(2).to_broadcast([4, 8, 128])
        Gsb = Gs[:, 0:8].unsqueeze(2).to_broadcast([4, 8, 128])
        tt("dve", v3(Wpp), v3(Ev), Geb, ALU.subtract, [rE, rS], [rWpp])
        act(Wpp, Wpp, AF.Exp, [rWpp], [rWpp])
        if own:
            tt("dve", v3(Wp), v3(Ev), Gsb, ALU.subtract, [rE, rS], [rWp])
            act(Wp, Wp, AF.Exp, [rWp], [rWp])
            tt("dve", v3(Rv), Gsb, v3(Gv), ALU.subtract, [rG, rS], [rR])
            act(Rv, Rv, AF.Exp, [rR], [rR])
            tt("dve", En, Bc, Gv, ALU.add, [rB, rG], [rEn])
            act(En, En, AF.Exp, [rEn], [rEn], scale=-1.0)
        tt("dve", tmpg[:, 0:8], Gs[:, 0:8], Ge[:, 0:8], ALU.subtract, [rS], [b("tmpg")])
        act(dec[:, 0:8], tmpg[:, 0:8], AF.Exp, [b("tmpg")], [b("dec")])
        tt("dve", mfo if own else mpre, Bc[:, 1023:1024], Gv[:, 1023:1024], ALU.add, [rB, rG], [b("mfo") if own else b("mpre")])
        pi = nps()
        t0 = 8 if own else 0
        qs = [(0, Wp, rWp), (1, Wpp, rWpp), (2, Rv, rR), (3, En, rEn)] if own else [(1, Wpp, rWpp)]
        for c in range(8):
            for q, rt, rb_ in qs:
                tr(PS[pi][:, c * 16 + q * 4: c * 16 + q * 4 + 4], rt[:, c * 128:(c + 1) * 128], ident_f[0:4, 0:4],
                   [rb_, b("K")], [PSB[pi]])
        pv = PS[pi][:, 0:128].rearrange("p (c q) -> p c q", q=16)
        if own:
            cp("dve", tok[:, t0:t0 + 8, :], pv, [PSB[pi]], [b("tok")], accum=True)
        else:
            cp("dve", tok[:, t0:t0 + 8, 4:8], pv[:, :, 4:8], [PSB[pi]], [b("tok")], accum=True)
        X = T1[:, 0:32].rearrange("p (h c) -> p h c", c=8)
        tt("dve", X, dec[:, 0:8].unsqueeze(1).to_broadcast([4, 4, 8]), E4.unsqueeze(2).to_broadcast([4, 4, 8]), ALU.mult,
           [b("dec"), b("K")], [rT])
        pj = nps()
        mm(PS[pj][:, 0:32], ones4, T1[:, 0:32], True, True, [rT, b("K")], [PSB[pj]])
        cp("dve", dbo if own else dbp, PS[pj][:, 0:32], [PSB[pj]], [b("dbo") if own else b("dbp")])

    pre_slots = []
    for c0 in (1024, 1536, 2048):
        si = wslot()
        wload(Wv[si], w_in[:, c0:c0 + 512], si)
        pre_slots.append(si)
    seg_rows(0, False)
    tt("dve", m_in, mpre, flag[0:4, :], ALU.mult, [b("mpre"), b("flag")], [b("m_in")])
    seg_rows(1024, True)
    dma("sp", ods(), mpo, mfo, [b("mfo")], [])

    def seg_sample():
        c0 = 2048
        lfs = lf[:, c0:c0 + 128]; igs = ig[:, c0:c0 + 128]
        B_, E_, G_, T_, Wp_, Wpp_, R_, En_ = [a[:, 0:128] for a in (Bc, Ev, Gv, T1, Wp, Wpp, Rv, En)]
        v3 = lambda a: a.rearrange("p (j t) -> p j t", t=8)
        scan(B_, segmul, lfs, 0.0, ALU.mult, ALU.add, [b("K"), b("lf")], [rB])
        tt("dve", E_, igs, B_, ALU.subtract, [rB, b("ig")], [rE])
        cp("dve", T_, E_, [rE], [rT])
        tt("dve", v3(T_)[:, :, 0], v3(E_)[:, :, 0], msr, ALU.max, [rE, b("msr")], [rT], accum=True)
        scan(G_, segadd, T_, 0.0, ALU.add, ALU.max, [rT, b("K")], [rG])
        cp("dve", Ge, v3(G_)[:, :, 7], [rG], [rS])
        Geb = Ge.unsqueeze(2).to_broadcast([4, 16, 8])
        Gsb = msr.unsqueeze(2).to_broadcast([4, 16, 8])
        tt("dve", v3(Wpp_), v3(E_), Geb, ALU.subtract, [rE, rS], [rWpp])
        act(Wpp_, Wpp_, AF.Exp, [rWpp], [rWpp])
        tt("dve", v3(Wp_), v3(E_), Gsb, ALU.subtract, [rE, b("msr")], [rWp])
        act(Wp_, Wp_, AF.Exp, [rWp], [rWp])
        tt("dve", v3(R_), Gsb, v3(G_), ALU.subtract, [rG, b("msr")], [rR])
        act(R_, R_, AF.Exp, [rR], [rR])
        tt("dve", En_, B_, G_, ALU.add, [rB, rG], [rEn])
        act(En_, En_, AF.Exp, [rEn], [rEn], scale=-1.0)
        tt("dve", tmpg, msr, Ge, ALU.subtract, [b("msr"), rS], [b("tmpg")])
        act(dec, tmpg, AF.Exp, [b("tmpg")], [b("dec")])
        tt("dve", mfs, v3(B_)[:, :, 7], v3(G_)[:, :, 7], ALU.add, [rB, rG], [b("mfs")])
        dma("sp", ods(), mso.rearrange("j h -> h j"), mfs, [b("mfs")], [], nc_ok=True)
        pi = nps()
        for q, rt, rb_ in [(0, Wp_, rWp), (1, Wpp_, rWpp), (2, R_, rR), (3, En_, rEn)]:
            tr(PS[pi][:, q * 4: q * 4 + 4], rt, ident_f[0:4, 0:4], [rb_, b("K")], [PSB[pi]])
        cp("dve", tok[:, 16, :], PS[pi][:, 0:16], [PSB[pi]], [b("tok")], accum=True)
        X = T1[:, 128:192].rearrange("p (h c) -> p h c", c=16)
        tt("dve", X, dec.unsqueeze(1).to_broadcast([4, 4, 16]), E4.unsqueeze(2).to_broadcast([4, 4, 16]), ALU.mult,
           [b("dec"), b("K")], [b("rT2")])
        pj = nps()
        mm(PS[pj][:, 0:64], ones4, T1[:, 128:192], True, True, [b("rT2"), b("K")], [PSB[pj]])
        cp("dve", dbs, PS[pj][:, 0:64], [PSB[pj]], [b("dbs")])
        pk = nps()
        mm(PS[pk][0:16, 0:4], sellast, tok[:, 16, 8:12], True, True, [b("K"), b("tok")], [PSB[pk]])
        cp("dve", dsj, PS[pk][0:16, 0:4], [PSB[pk]], [b("dsj")])

    dump("ig", ig, [b("ig")], 4)
    dump("lf", lf, [b("lf")], 4)
    seg_sample()
    dump("tok", tok.rearrange("p a b -> p (a b)"), [b("tok")])
    dump("dbp", dbp, [b("dbp")]); dump("dbo", dbo, [b("dbo")]); dump("dbs", dbs, [b("dbs")])
    dump("misc", misc, [b("mfs"), b("mfo"), b("mpre"), b("m_in"), b("dec"), b("msr")])
    dump("adaT", adaT.rearrange("p a b -> p (a b)"), [b("ada%d" % i) for i in range(6)])
    dump("sc1", scale1T.rearrange("p a b -> p (a b)"), [b("sc1")])
    dump("Bs", Bc[:, 0:128], [rB], 4); dump("Es", Ev[:, 0:128], [rE], 4); dump("Gs_", Gv[:, 0:128], [rG], 4)
    hp7 = V(84384, [16, 16])
    cp("dve", hp7, hT_pre[:, :, 1008:1024], [hbuf(7)], [b("hp7"), b("xn")])
    p.barrier()

    chk(4)
    O_KPRE = 156672; O_VPRE = 173056
    O_KW = 193536; O_CT = 195584; O_CTB = 203840
    kpre = V(O_KPRE, [8, 1024]); vpre = V(O_VPRE, [8, 4, 258])
    kw = [V(O_KW + 512 * i, [256]) for i in range(4)]
    CT = [V(O_CT + 2064 * h, [2, 258], F32) for h in range(4)]
    CTb = [V(O_CTB + 1032 * h, [2, 258]) for h in range(4)]
    memset("dve", vpre[:, :, :, 256:257], 1.0, [b("vpre")])
    for h in range(4):
        memset("dve", CT[h], 0.0, [b("CT%d" % h)])
    for bi_, c0 in enumerate((1024, 1536, 2048, 2560)):
        if bi_ < 3:
            si = pre_slots[bi_]
        else:
            si = wslot()
            wload(Wv[si], w_in[:, c0:c0 + 512], si)
        for t in range(8):
            pi = nps()
            for k in range(16):
                mm(PS[pi][:, :], hT_pre[:, k, t * 128:(t + 1) * 128], Wv[si][:, k, :], k == 0, k == 15,
                   [hbuf(t), WB[si]], [PSB[pi]])
            if bi_ < 2:
                ts("dve", kpre[:, t, bi_ * 512:(bi_ + 1) * 512], PS[pi][:, :], 0.0625, None, ALU.mult, None, [PSB[pi]],
                   [b("kpre%d" % t)], accum=True)
            else:
                vb = bi_ - 2
                cp("act" if t % 2 == 0 else "dve", vpre[:, t, 2 * vb:2 * vb + 2, 0:256],
                   PS[pi][:, :].rearrange("p (a c) -> p a c", c=256), [PSB[pi]], [b("vpre")], accum=True)
    def load_w(h, sa_, sv_):
        wload(Wv[sa_][:, :, 0:256], w_in[:, h * 256:(h + 1) * 256], sa_)
        wload(Wv[sa_][:, :, 256:512], w_in[:, 1024 + h * 256:1024 + (h + 1) * 256], sa_, accum=True)
        wload(Wv[sv_][:, :, 0:256], w_in[:, 2048 + h * 256:2048 + (h + 1) * 256], sv_)
        wload(Wv[sv_][:, :, 256:512], w_in[:, 3072 + h * 256:3072 + (h + 1) * 256], sv_, accum=True)

    s_a = wslot(); s_v = wslot()
    s_f = ({0, 1, 2} - {s_a, s_v}).pop()
    load_w(0, s_a, s_v)
    ada_cur = ada_rest.pop(0)
    ada_load(ada_cur, s_f)
    steps = [(c, h) for c in range(8) for h in range(4)]

    def KWp(i):
        c, h = steps[i]
        r = i % 4
        ts("dve", kw[r], kpre[:, c, h * 256:(h + 1) * 256], tok[:, c, 4 + h:5 + h], None, ALU.mult, None,
           [b("kpre%d" % c), b("tok")], [b("kw%d" % r)])

    for i in range(3):
        KWp(i)
    for i, (c, h) in enumerate(steps):
        r = i % 4
        pis = []
        for d_ in range(2):
            pi = nps()
            pis.append(pi)
            mm(PS[pi][:, 0:257], kw[r][:, d_ * 128:(d_ + 1) * 128], vpre[:, c, h, 0:257], True, True,
               [b("kw%d" % r), b("vpre")], [PSB[pi]])
        if i + 3 < len(steps):
            KWp(i + 3)
        for d_ in range(2):
            pi = pis[d_]
            stt(CT[h][:, d_, 0:257], CT[h][:, d_, 0:257], dbp[:, h * 8 + c:h * 8 + c + 1], PS[pi][:, 0:257],
                ALU.mult, ALU.add, [b("CT%d" % h), b("dbp"), PSB[pi]], [b("CT%d" % h)])
    for h in range(4):
        ts("dve", CT[h], CT[h], flag, None, ALU.mult, None, [b("CT%d" % h), b("flag")], [b("CT%d" % h)])
    p.barrier()

    chk(5)
    O_QT = 87040; O_KT = 91648; O_OTS = 96256; O_KTOK = 100864; O_VEXT = 105472
    O_SP = 110144; O_HMN = 110656; O_NTB = 111680; O_NIN = 111936; O_NROWS = 116032
    O_CAT = 156672
    O_QBD = 66560; O_CN = 74752; O_CTE = 80896; O_VW = 82976; O_WM = 84000
    qT = V(O_QT, [2, 1152]); kT = V(O_KT, [2, 1152]); oTs = V(O_OTS, [2, 1152])
    ktok = V(O_KTOK, [9, 256]); vext = V(O_VEXT, [9, 258])
    Sp = [V(O_SP + 256 * i, [128]) for i in range(2)]
    hmn = [V(O_HMN + 512 * i, [256]) for i in range(2)]
    nT_b = V(O_NTB, [2, 64])
    nin = V(O_NIN, [1024], F32, parts=16)
    nrows = V(O_NROWS, [256], F32, parts=64)
    catT = V(O_CAT, [16, 1152])
    QBD = V(O_QBD, [2, 2048])
    Cn = [V(O_CN + 2048 * i, [2, 256], F32) for i in range(3)] + [V(84896, [2, 256], F32), V(117056, [2, 256], F32)]
    NCN = 5
    CTe = [V(O_CTE + 1040 * i, [2, 258]) for i in range(2)] + [V(207968, [2, 258]), V(209008, [2, 258])]
    NCE = 4
    vw = [V(O_VW + 512 * i, [256]) for i in range(2)] + [V(210048, [256])]
    NVW = 3
    wm_b = V(O_WM, [4, 16])
    wm_f = V(O_WM + 128, [4, 16], F32)
    Sps = V(119104, [128])
    CTbp = [V(O_CTB + 1032 * i, [2, 258]) for i in range(2)]
    rotn[0] = 5; rot[0] = 0
    PSA_C = [5, 7]; PSA_S = 6

    memset("dve", QBD, 0.0, [b("QBD")])
    memset("dve", vext[:, :, 256:257], 1.0, [b("vext_ones")])
    dcs = [p.dsem("cn%d" % i) for i in range(NCN)]
    dcs_st = [p.dsem("cns%d" % i) for i in range(NCN)]
    dma("sp", one(), nrows, ns, [], [b("nrows")])
    dma("sp", one(), nin, ns.rearrange("(j h) k -> j (h k)", h=4), [], [b("nin")])
    pi = nps()
    for kc in range(2):
        tr(PS[pi][:, kc * 64:(kc + 1) * 64], nrows[0:64, kc * 128:(kc + 1) * 128], ident_f[0:64, 0:64],
           [b("nrows"), b("K")], [PSB[pi]])
    cp("dve", nT_b, PS[pi][:, 0:128].rearrange("p (a c) -> p a c", c=64), [PSB[pi]], [b("nTb")])
    tt("dve", wm_f, tok[:, 16, 4:8].unsqueeze(2).to_broadcast([128, 4, 16]),
       BD16.unsqueeze(1).to_broadcast([128, 4, 16]), ALU.mult, [b("tok"), b("K")], [b("wmf")])
    cp("dve", wm_b, wm_f, [b("wmf")], [b("wmb")])

    def diag_ap(base):
        return bass.AP(base.tensor, base.offset, [list(base.ap[0]), [136, 16], [1, 8]])

    def evac_chain(pa, t, h):
        sb_ = b("sml%d" % (t % 2))
        o_ = (t % 2) * 8
        c = lambda i: sml[:, o_ + i:o_ + i + 1]
        r_tok = tok[:, t, 8 + h:9 + h]; en_tok = tok[:, t, 12 + h:13 + h]
        r = t % 2
        act(c(0), PS[pa][:, 256:257], AF.Abs, [PSB[pa], b("tok")], [sb_], scale=r_tok)
        tt("dve", c(1), c(0), en_tok, ALU.max, [sb_, b("tok")], [sb_], accum=True)
        recip(c(2), c(1), [sb_], [sb_], accum=True)
        tt("dve", c(3), c(2), r_tok, ALU.mult, [sb_, b("tok")], [sb_], accum=True)
        act(hmn[r], PS[pa][:, 0:256], AF.Square, [PSB[pa], sb_], [b("hmn%d" % r), sb_], accum=True, scale=c(3), accum_out=c(4))
        act(c(6), c(4), AF.Sqrt, [sb_, b("eps")], [sb_], accum=True, scale=1.0 / 256.0, bias=epsc)
        recip(c(7), c(6), [sb_], [sb_], accum=True)
        tt("dve", c(7), c(7), c(3), ALU.mult, [sb_], [sb_], accum=True)
        ts("dve", hmn[r], PS[pa][:, 0:256], c(7), None, ALU.mult, None, [PSB[pa], sb_], [b("hmn%d" % r)])

    def evac_T(t, h, col0):
        r = t % 2
        pt = nps()
        for vc in range(2):
            tr(psb(pt)[:, vc * 128:(vc + 1) * 128], hmn[r][:, vc * 128:(vc + 1) * 128], ident_b,
               [b("hmn%d" % r), b("idb")], [PSB[pt]])
        for vc in range(2):
            stt(catT[:, h * 2 + vc, col0:col0 + 128], psb(pt)[:, vc * 128:(vc + 1) * 128],
                g_headT[:, h * 2 + vc:h * 2 + vc + 1], oTs[:, vc, col0:col0 + 128], ALU.mult, ALU.mult,
                [PSB[pt], b("vecT"), b("oTs")], [b("catT")], accum=True)

    def sample_gen(h):
        def load(j):
            ci = j % NCN
            dma("sp", dcs[ci], Cn[ci], Cs[j * 4 + h].rearrange("(vc p) k -> p vc k", p=128), [], [b("Cn%d" % ci)])

        def T(j):
            ci = j % NCN; e = j % NCE; pair = j * 4 + h
            pc = nps()
            for kc in range(2):
                for vc in range(2):
                    tr(PS[pc][:, kc * 256 + vc * 128: kc * 256 + (vc + 1) * 128], Cn[ci][:, vc, kc * 128:(kc + 1) * 128],
                       ident_f, [b("Cn%d" % ci), b("K")], [PSB[pc]])
            cp("act", CTe[e][:, :, 0:256], PS[pc][:, :].rearrange("p (a c) -> p a c", c=256), [PSB[pc]], [b("CTe%d" % e)])
            cp("dve", CTe[e][:, :, 256:257], nT_b[:, :, pair:pair + 1], [b("nTb")], [b("CTe%d" % e)], accum=True)

        for j in range(NCN):
            load(j)
        T(0)
        yield
        for j in range(16):
            ci = j % NCN; e = j % NCE; v_ = j % NVW; pair = j * 4 + h
            ts("dve", vw[v_], vext[:, 8, 0:256], wm_f[:, h, j:j + 1], None, ALU.mult, None, [b("vext8"), b("wmf")], [b("vw%d" % v_)])
            if j + 1 < 16:
                T(j + 1)
            for kc in range(2):
                mm(PS[PSA_S][:, 0:257], QBD[:, kc, j * 128:(j + 1) * 128], CTe[e][:, kc, 0:257], j == 0 and kc == 0, False,
                   [b("QBD"), b("CTe%d" % e)], [PSB[PSA_S]])
            pn = nps()
            for vc in range(2):
                mm(PS[pn][:, vc * 256:(vc + 1) * 256], vw[v_][:, vc * 128:(vc + 1) * 128], ktok[:, 8, :], True, True,
                   [b("vw%d" % v_), b("ktok8")], [PSB[pn]])
            stt(Cn[ci], Cn[ci], dbs[:, h * 16 + j:h * 16 + j + 1], PS[pn][:, :].rearrange("p (a c) -> p a c", c=256),
                ALU.mult, ALU.add, [b("Cn%d" % ci), b("dbs"), PSB[pn]], [b("Cn%d" % ci)])
            dma("sp", dcs_st[ci], Cso[pair].rearrange("(vc p) k -> p vc k", p=128), Cn[ci], [b("Cn%d" % ci)], [])
            if j + NCN < 16:
                load(j + NCN)
            yield

    dcpo = p.dsem("cpo")
    dnpo = p.dsem("npo")
    for h in range(4):
        sa, sv = s_a, s_v
        free_slot = s_f
        if h > 0:
            ada_cur = ada_rest.pop(0)
            ada_load(ada_cur, s_f)
        sg = sample_gen(h)
        next(sg)
        hall = [hbuf(t) for t in range(8, 17)]
        n_ev = 0
        for (dst, si, off, kind, dbuf) in ((qT, sa, 0, 0, "qT"), (kT, sa, 256, 0, "kT"), (oTs, sv, 256, 1, "oTs")):
            for cc in range(2):
                for tb in range(3):
                    pi = nps()
                    for k in range(16):
                        mm(PS[pi][:, 0:384], Wv[si][:, k, off + cc * 128: off + (cc + 1) * 128],
                           hT_own[:, k, tb * 384:(tb + 1) * 384], k == 0, k == 15, hall + [WB[si]], [PSB[pi]])
                    d = dst[:, cc, tb * 384:(tb + 1) * 384]
                    if kind == 1:
                        act(d, PS[pi][:, 0:384], AF.Sigmoid, [PSB[pi]], [b(dbuf)], accum=True)
                    else:
                        sc_ = 0.0625 if dbuf == "kT" else 1.0
                        if n_ev % 2 == 0:
                            act(d, PS[pi][:, 0:384], AF.Copy, [PSB[pi]], [b(dbuf)], accum=True, scale=sc_)
                        else:
                            ts("dve", d, PS[pi][:, 0:384], sc_, None, ALU.mult, None, [PSB[pi]], [b(dbuf)], accum=True)
                        n_ev += 1
        for t in range(9):
            pi = nps(); pv_ = nps()
            for k in range(16):
                mm(PS[pi][:, 0:256], hT_own[:, k, t * 128:(t + 1) * 128], Wv[sa][:, k, 256:512], k == 0, k == 15,
                   [hbuf(8 + t), WB[sa]], [PSB[pi]])
            for k in range(16):
                mm(PS[pv_][:, 0:256], hT_own[:, k, t * 128:(t + 1) * 128], Wv[sv][:, k, 0:256], k == 0, k == 15,
                   [hbuf(8 + t), WB[sv]], [PSB[pv_]])
            act(ktok[:, t, :], PS[pi][:, 0:256], AF.Copy, [PSB[pi]], [b("ktok%d" % t)], scale=0.0625)
            cp("dve", vext[:, t, 0:256], PS[pv_][:, 0:256], [PSB[pv_]], [b("vext%d" % t)])
        ada_mm(ada_cur, s_f)
        ada_cur = ada_rest.pop(0)
        ada_load(ada_cur, s_f)
        if h < 3:
            load_w(h + 1, s_a, s_v)
        else:
            su = [s_a, s_v]
            wload(Wv[su[0]], w_in[:, 4104:4616], su[0])
            wload(Wv[su[1]], w_in[:, 4616:5128], su[1])
        scol = slice(1024, 1152)
        cp("act", CTbp[0], CT[h], [b("CT%d" % h)], [b("CTbp0")])
        ps_ = nps()
        for d_ in range(2):
            mm(PS[ps_][:, 0:128], kT[:, d_, scol], qT[:, d_, scol], d_ == 0, d_ == 1, [b("kT"), b("qT")], [PSB[ps_]])
        stt(Sps, PS[ps_][:, 0:128], tok[:, 16, h:h + 1], maskBD_b, ALU.mult, ALU.mult,
            [PSB[ps_], b("tok"), b("mbb")], [b("Sps")])
        for d_ in range(2):
            cp("dve", diag_ap(QBD[:, d_, :]), qT[:, d_, scol].rearrange("p (j t) -> p j t", t=8), [b("qT")], [b("QBD")],
               accum=(d_ == 1))
        def ST(c):
            cols = slice(c * 128, (c + 1) * 128)
            ps2 = nps()
            for d_ in range(2):
                mm(PS[ps2][:, 0:128], kT[:, d_, cols], qT[:, d_, cols], d_ == 0, d_ == 1, [b("kT"), b("qT")], [PSB[ps2]])
            stt(Sp[c % 2], PS[ps2][:, 0:128], tok[:, 8 + c, h:h + 1], maskC_b, ALU.mult, ALU.mult,
                [PSB[ps2], b("tok"), b("mcb")], [b("Sp%d" % (c % 2))])

        def KW(c):
            rk = c % 4
            ts("dve", kw[rk], ktok[:, c, :], tok[:, 8 + c, 4 + h:5 + h], None, ALU.mult, None,
               [b("ktok%d" % c), b("tok")], [b("kw%d" % rk)])

        ST(0); KW(0)
        for c in range(8):
            t = 8 + c
            cols = slice(c * 128, (c + 1) * 128)
            pa = PSA_C[c % 2]
            cb_ = b("CTbp%d" % (c % 2))
            for d_ in range(2):
                mm(PS[pa][:, 0:257], qT[:, d_, cols], CTbp[c % 2][:, d_, 0:257], d_ == 0, False, [b("qT"), cb_], [PSB[pa]])
            mm(PS[pa][:, 0:257], Sp[c % 2], vext[:, c, 0:257], False, True, [b("Sp%d" % (c % 2)), b("vext%d" % c), b("vext_ones")], [PSB[pa]])
            rk = c % 4
            for d_ in range(2):
                pu = nps()
                mm(PS[pu][:, 0:257], kw[rk][:, d_ * 128:(d_ + 1) * 128], vext[:, c, 0:257], True, True,
                   [b("kw%d" % rk), b("vext%d" % c), b("vext_ones")], [PSB[pu]])
                stt(CT[h][:, d_, 0:257], CT[h][:, d_, 0:257], dbo[:, h * 8 + c:h * 8 + c + 1], PS[pu][:, 0:257],
                    ALU.mult, ALU.add, [b("CT%d" % h), b("dbo"), PSB[pu]], [b("CT%d" % h)])
            if c < 7:
                cp("act", CTbp[(c + 1) % 2], CT[h], [b("CT%d" % h)], [b("CTbp%d" % ((c + 1) % 2))])
                ST(c + 1); KW(c + 1)
            evac_chain(pa, t, h)
            if c > 0:
                evac_T(t - 1, h, (c - 1) * 128)
            for _ in range(2):
                next(sg, None)
            if c == 3:
                ada_mm(ada_cur, s_f)
                ada_cur = ada_rest.pop(0)
                ada_load(ada_cur, s_f)
            if c == 7:
                ada_mm(ada_cur, s_f)
        evac_T(15, h, 7 * 128)
        po = nps()
        for d_ in range(2):
            for vc in range(2):
                tr(PS[po][:, vc * 256 + d_ * 128: vc * 256 + (d_ + 1) * 128], CT[h][:, d_, vc * 128:(vc + 1) * 128],
                   ident_f, [b("CT%d" % h), b("K")], [PSB[po]])
        cpo = V(O_KW, [2, 256], F32)
        cp("dve", cpo, PS[po][:, :].rearrange("p (a c) -> p a c", c=256), [PSB[po]], [b("kw0"), b("kw1"), b("kw2"), b("kw3")])
        dma("pool", dcpo, Cp[h].rearrange("(vc p) d -> p vc d", p=128), cpo, [b("kw0"), b("kw1"), b("kw2"), b("kw3")], [])
        dma("pool", dnpo, npo[h].rearrange("(d p o) -> p d o", p=128, o=1), CT[h][:, :, 256:257], [b("CT%d" % h)], [], nc_ok=True)
        for _ in sg:
            pass
        mm(PS[PSA_S][:, 0:257], Sps, vext[:, 8, 0:257], False, True, [b("Sps"), b("vext8"), b("vext_ones")], [PSB[PSA_S]])
        evac_chain(PSA_S, 16, h)
        evac_T(16, h, 1024)
        pq = nps()
        mm(PS[pq][0:16, 0:256], wm_b[:, h, :], ktok[:, 8, :], True, True, [b("wmb"), b("ktok8")], [PSB[pq]])
        stt(nin[:, h * 256:(h + 1) * 256], nin[:, h * 256:(h + 1) * 256], dsj[:, h:h + 1], PS[pq][0:16, 0:256],
            ALU.mult, ALU.add, [b("nin"), b("dsj"), PSB[pq]], [b("nin")])
    rotn[0] = 6; rot[0] = 0
    assert not ada_rest
    stt(scale2T, adaT[:, 64:80, :], 1.0, g_ffnT.unsqueeze(2).to_broadcast([128, 16, 17]), ALU.add, ALU.mult,
        [b("ada4"), b("vecT")], [b("sc2")])
    dma("sp", ods(), nso.rearrange("(j h) k -> j (h k)", h=4), nin, [b("nin")], [])
    p.barrier()

    chk(6)
    O_UEXT = 87040; O_USX = 91648; O_PA = 93184; O_PB = 97792; O_PSA = 102400; O_PSB_ = 103936
    O_POOLED = 105472; O_UTOK = 110080; O_UTOKS = 114176; O_USC = 118272
    O_BR = 66560; O_WP = 74752 + 8192 - 8192
    uext = V(O_UEXT, [1152], F32); usx = V(O_USX, [16, 23], F32)
    pa_ = V(O_PA, [1152], F32); pb2 = V(O_PB, [1152], F32)
    psa = V(O_PSA, [16, 23], F32); psb2 = V(O_PSB_, [16, 23], F32)
    pooled = V(O_POOLED, [2, 1152])
    utok = V(O_UTOK, [1024], F32); utoks = V(O_UTOKS, [1024], F32)
    usc = V(O_USC, [128], F32)
    br = [V(O_BR + 4096 * i, [1024], F32, parts=120) for i in range(2)]
    wp = V(O_BR + 8192, [4, 2, 256])
    for i in range(2):
        dma("sp", one(), br[i], bufs[i * 8:(i + 1) * 8].rearrange("j r c -> (j r) c"), [], [b("br%d" % i)])
    dwp = p.dsem("wp")
    for g in range(4):
        dma("pool", dwp, wp[:, g, :, :], w_pool[g].rearrange("(cc p) d -> p cc d", p=128), [], [b("wp")], accum=True)
    dma("sp", ods(), bufso[:, 0:7, :], bufs[:, 8:15, :], [], [])
    memset("dve", uext[:, 0:112], 0.0, [b("uext0")])
    ublocks = [(hp7, 0, 16, [b("hp7")]), (hT_own, 0, 512, [hbuf(t) for t in range(8, 12)]),
               (hT_own, 512, 512, [hbuf(t) for t in range(12, 16)]), (hT_own, 1024, 128, [hbuf(16)])]
    for g in range(4):
        w = POOL_W[g]
        for cc in range(2):
            cu = g * 2 + cc
            si = su[cu // 4]
            wc = (cu % 4) * 128
            for bi_, (hb_, c0, n, rd) in enumerate(ublocks):
                pi = nps()
                for k in range(16):
                    mm(PS[pi][:, 0:n], Wv[si][:, k, wc:wc + 128], hb_[:, k, c0:c0 + n], k == 0, k == 15, rd + [WB[si]], [PSB[pi]])
                if bi_ == 0:
                    ts("dve", uext[:, 112:128], PS[pi][:, 0:16], flag, None, ALU.mult, None, [PSB[pi], b("flag")], [b("uext")])
                elif bi_ < 3:
                    cp("act", uext[:, 128 + c0:128 + c0 + n], PS[pi][:, 0:n], [PSB[pi]], [b("uext")], accum=True)
                else:
                    cp("act", usc, PS[pi][:, 0:128], [PSB[pi]], [b("usc")])
            cp("dve", usx[:, :, 15:23], usc.rearrange("p (j t) -> p j t", t=8), [b("usc")], [b("usx")])
            pi = nps()
            for hf_ in range(2):
                tr(PS[pi][:, hf_ * 120:(hf_ + 1) * 120], br[hf_][0:120, cu * 128:(cu + 1) * 128], ident_f[0:120, 0:120],
                   [b("br%d" % hf_), b("K")], [PSB[pi]])
            cp("dve", usx[:, :, 0:15], PS[pi][:, 0:240].rearrange("p (j r) -> p j r", r=15), [PSB[pi]], [b("usx")], accum=True)
            pi = nps()
            tr(PS[pi][:, 0:128], uext[:, 1024:1152], ident_f, [b("uext"), b("K")], [PSB[pi]])
            tr(PS[pi][:, 128:256], usc, ident_f, [b("usc"), b("K")], [PSB[pi]])
            cp("act", utok[:, cu * 128:(cu + 1) * 128], PS[pi][:, 0:128], [PSB[pi]], [b("utok")], accum=True)
            cp("act", utoks[:, cu * 128:(cu + 1) * 128], PS[pi][:, 128:256], [PSB[pi]], [b("utoks")], accum=True)
            cur, curs = uext, usx
            bufsP = [pa_, pb2]; bufsS = [psa, psb2]
            L = 0; sh = 1; it = 0
            while sh < w:
                L += sh
                nx, nxs = bufsP[it % 2], bufsS[it % 2]
                tt("dve", nx[:, L:1152], cur[:, L:1152], cur[:, L - sh:1152 - sh], ALU.add, [b("uext"), b("uext0"), b("pp")], [b("pp")])
                tt("dve", nxs[:, :, L:23], curs[:, :, L:23], curs[:, :, L - sh:23 - sh], ALU.add, [b("usx"), b("pps")], [b("pps")])
                cur, curs = nx, nxs
                sh *= 2; it += 1
            stt(pooled[:, cc, 0:1024], cur[:, 128:1152], 1.0 / w, uext[:, 128:1152], ALU.mult, ALU.subtract,
                [b("pp"), b("uext")], [b("pooled")], accum=(cc == 1))
            t16 = sml[:, 16:32]
            tt("dve", t16, cur[:, 128:144], invtab[:, g * 16:(g + 1) * 16], ALU.mult, [b("pp"), b("K")], [b("t16")])
            tt("dve", pooled[:, cc, 0:16], t16, uext[:, 128:144], ALU.subtract, [b("t16"), b("uext")], [b("pooled")], accum=True)
            stt(pooled[:, cc, 1024:1152].rearrange("p (j t) -> p j t", t=8), curs[:, :, 15:23], 1.0 / w, usx[:, :, 15:23],
                ALU.mult, ALU.subtract, [b("pps"), b("usx")], [b("pooled")], accum=True)
        for dd in range(2):
            for tb in range(3):
                pi = nps()
                for cc in range(2):
                    mm(PS[pi][:, 0:384], wp[:, g, cc, dd * 128:(dd + 1) * 128], pooled[:, cc, tb * 384:(tb + 1) * 384],
                       cc == 0, cc == 1, [b("wp"), b("pooled")], [PSB[pi]])
                act(catT[:, 8 + g * 2 + dd, tb * 384:(tb + 1) * 384], PS[pi][:, 0:384], AF.Copy, [PSB[pi], b("vecT")],
                    [b("catT")], accum=True, scale=pool_scaleT[:, g * 2 + dd:g * 2 + dd + 1])
    dma("sp", ods(), bufp, utok[113:128, :], [b("utok")], [])
    for j in range(16):
        dma("sp", ods(), bufso[j, 7:15, :], utoks[j * 8:(j + 1) * 8, :], [b("utoks")], [])
    dump("uext", uext, [b("uext")])
    dump("pa", pa_, [b("pp")])
    dump("pb", pb2, [b("pp")])
    dump("pooled", V(O_POOLED, [1152], F32), [b("pooled")])
    dump("catT", V(O_CAT, [9216], F32), [b("catT")])
    dump("oTs", V(O_OTS, [1152], F32), [b("oTs")])
    dump("qT", V(O_QT, [1152], F32), [b("qT")])
    dump("kT", V(O_KT, [1152], F32), [b("kT")])
    p.barrier()

    chk(7)
    O_W3 = 87040; O_GP = 103424; O_GS = 111616; O_TMP = 193536; O_EX = 195584
    Wout = [Wv[0], Wv[1], Wv[2], V(O_W3, [16, 512])]
    WoB = [WB[0], WB[1], WB[2], b("W3")]
    wod = [wds[0], wds[1], wds[2], p.dsem("w3")]
    for cb in range(4):
        dma("pool", wod[cb], Wout[cb], w_out[:, cb * 512:(cb + 1) * 512].rearrange("(k p) n -> p k n", p=128), [], [WoB[cb]])
    wrot[0] = 0
    h2T = V(O_HTO, [16, 1152])
    tmpf = V(O_TMP, [512], F32)
    tmpf2 = [tmpf, V(196608, [512], F32)]
    ex = [V(O_EX + 512 * i, [128], F32) for i in range(2)]

    def gt_cols(chunk0, nchunk, which, dst, dstbuf):
        pi = nps()
        for i in range(nchunk):
            e_ = i % 2
            if which == 0:
                cp("dve", ex[e_], adaT[:, chunk0 + i, 0:1].to_broadcast([128, 128]), [b("ada2"), b("ada5")], [b("ex%d" % e_)])
            else:
                cp("dve", ex[e_].rearrange("p (j t) -> p j t", t=8),
                   adaT[:, chunk0 + i, 1:17].unsqueeze(2).to_broadcast([128, 16, 8]), [b("ada2"), b("ada5")], [b("ex%d" % e_)])
            tr(PS[pi][:, (i % 4) * 128:(i % 4 + 1) * 128], ex[e_], ident_f, [b("ex%d" % e_), b("K")], [PSB[pi]])
            if i % 4 == 3 or i == nchunk - 1:
                n0 = (i // 4) * 4
                cp("act", dst[:, n0 * 128:(i + 1) * 128], PS[pi][:, 0:(i - n0 + 1) * 128], [PSB[pi]], [dstbuf], accum=True)
                if i != nchunk - 1:
                    pi = nps()

    gtp = V(O_GP, [2048], F32); gts = V(O_GS, [2048], F32)
    gt_cols(32, 16, 0, gtp, b("gtp"))
    gt_cols(32, 16, 1, gts, b("gts"))
    def p6_mm(t):
        s_ = t % 2
        src = xo[t * 128:(t + 1) * 128, :] if t < 8 else xs
        dma("sp", dxs[s_], xt[s_], src, [], [b("xt%d" % s_)])
        gtb, gtbuf = (gtp, b("gtp")) if t < 8 else (gts, b("gts"))
        for cb in range(4):
            pi = nps()
            for k in range(16):
                mm(PS[pi][:, :], catT[:, k, t * 128:(t + 1) * 128], Wout[cb][:, k, :], k == 0, k == 15,
                   [b("catT"), WoB[cb]], [PSB[pi]])
            kk = (t * 4 + cb) % 2
            tq = tmpf2[kk]
            tt("dve", tq, PS[pi][:, :], gtb[:, cb * 512:(cb + 1) * 512], ALU.mult, [PSB[pi], gtbuf], [b("tmpf%d" % kk)])
            tt("pool", xt[s_][:, cb * 512:(cb + 1) * 512], tq, xt[s_][:, cb * 512:(cb + 1) * 512], ALU.add,
               [b("tmpf%d" % kk), b("xt%d" % s_)], [b("xt%d" % s_)])

    p6_mm(0)
    for t in range(9):
        s = t % 2
        if t + 1 < 9:
            p6_mm(t + 1)
        norm_to_T(t, b("xt%d" % s), xt[s], h2T, t * 128, scale2T, 48, 0, t == 8, b("sc2"), b("ada3"), b("h2T"),
                  O_XT[1])
        dma("pool", dxs_st[s], x1s[t * 128:(t + 1) * 128, :], xt[s], [b("xt%d" % s)], [b("x1s%d" % t)])
    p.barrier()

    chk(8)
    O_ACTA = 50176; O_ACTB = 156672; O_SA = 188928

    def actT(j):
        if j < 30:
            return V(O_ACTA + 2304 * j, [1152])
        return V(O_ACTB + 2304 * (j - 30), [1152])

    sa_t = [V(O_SA + 1536 * i, [384], F32) for i in range(2)]
    fslots = [0, 1]
    fr = [0]
    for s in range(22):
        si = fslots[fr[0]]; fr[0] = (fr[0] + 1) % 2
        j0 = 2 * s
        wload(Wv[si][:, :, 0:256], w_ffn_in[:, j0 * 128:j0 * 128 + 256], si)
        wload(Wv[si][:, :, 256:512], w_ffn_in[:, 5632 + j0 * 128:5632 + j0 * 128 + 256], si, accum=True)
        for jj in range(2):
            j = j0 + jj
            for tb in range(3):
                pa = nps(); pb_ = nps()
                for k in range(16):
                    mm(PS[pa][:, 0:384], Wv[si][:, k, jj * 128:(jj + 1) * 128], h2T[:, k, tb * 384:(tb + 1) * 384],
                       k == 0, k == 15, [b("h2T"), WB[si]], [PSB[pa]])
                for k in range(16):
                    mm(PS[pb_][:, 0:384], Wv[si][:, k, 256 + jj * 128:256 + (jj + 1) * 128], h2T[:, k, tb * 384:(tb + 1) * 384],
                       k == 0, k == 15, [b("h2T"), WB[si]], [PSB[pb_]])
                e_ = tb % 2
                act(sa_t[e_], PS[pa][:, 0:384], AF.Silu, [PSB[pa]], [b("sa%d" % e_)])
                tt("dve", actT(j)[:, tb * 384:(tb + 1) * 384], sa_t[e_], PS[pb_][:, 0:384], ALU.mult,
                   [b("sa%d" % e_), PSB[pb_]], [b("actT")], accum=True)
    p.barrier()

    chk(9)
    O_SLA = 17408; O_SLB = 119808; O_XP = 142336; O_G2 = 146432; O_TM2 = 150528
    fsl = [V(O_SLA, [44, 256]), V(O_SLB, [44, 256])]
    fsb = [b("FA"), b("FB")]
    fds = [p.dsem("fa"), p.dsem("fb")]
    xp = [V(O_XP + 1024 * i, [256], F32) for i in range(4)]
    g2 = [[V(O_G2 + 2048 * i + 1024 * w_, [256], F32) for w_ in range(2)] for i in range(2)]
    tm2 = V(O_TM2, [256], F32)
    dxp = [p.dsem("xp%d" % i) for i in range(4)]
    dxp_st = [p.dsem("xps%d" % i) for i in range(4)]
    xr = [0]
    def fload_w(cb):
        si = cb % 2
        dma("pool", fds[si], fsl[si], w_ffn_out[:, cb * 256:(cb + 1) * 256].rearrange("(c p) n -> p c n", p=128), [], [fsb[si]])

    fload_w(0)
    for cb in range(8):
        si = cb % 2
        if cb + 1 < 8:
            fload_w(cb + 1)
        gt_cols(80 + 2 * cb, 2, 0, g2[si][0], b("g2p%d" % si))
        gt_cols(80 + 2 * cb, 2, 1, g2[si][1], b("g2s%d" % si))
        for t in range(9):
            r = xr[0]; xr[0] = (r + 1) % 4
            dma("sp", dxp[r], xp[r], x1s[t * 128:(t + 1) * 128, cb * 256:(cb + 1) * 256], [b("x1s%d" % t)], [b("xp%d" % r)])
            pi = nps()
            for j in range(44):
                mm(PS[pi][:, 0:256], actT(j)[:, t * 128:(t + 1) * 128], fsl[si][:, j, :], j == 0, j == 43,
                   [b("actT"), fsb[si]], [PSB[pi]])
            w_ = 0 if t < 8 else 1
            tt("dve", tm2, PS[pi][:, 0:256], g2[si][w_], ALU.mult, [PSB[pi], b("g2p%d" % si), b("g2s%d" % si)], [b("tm2")])
            tt("dve", xp[r], tm2, xp[r], ALU.add, [b("tm2"), b("xp%d" % r)], [b("xp%d" % r)])
            dma("pool", dxp_st[r], x2s[t * 128:(t + 1) * 128, cb * 256:(cb + 1) * 256], xp[r], [b("xp%d" % r)], [b("x2s%d" % t)], accum=True)
    p.barrier()

    chk(10)
    O_XF = [50176, 58368, 66560]; O_GF = 74752; O_JK = 82944
    xf = [V(o, [2048], F32) for o in O_XF]
    gfin = V(O_GF, [2048], F32); junk = V(O_JK, [2048])
    dma("sp", one(), gfin, g_final.partition_broadcast(128), [], [b("gfin")])
    dxf = [p.dsem("xf%d" % i) for i in range(3)]
    dxf_st = [p.dsem("xfs%d" % i) for i in range(3)]
    def fload(t):
        r = t % 3
        dma("sp", dxf[r], xf[r], x2s[t * 128:(t + 1) * 128, :], [b("x2s%d" % t)], [b("xf%d" % r)])

    for t in range(3):
        fload(t)
    for t in range(9):
        r = t % 3
        s0 = ssq[:, (t % 2) * 4 + 0:(t % 2) * 4 + 1]
        s1 = ssq[:, (t % 2) * 4 + 1:(t % 2) * 4 + 2]
        s2 = ssq[:, (t % 2) * 4 + 2:(t % 2) * 4 + 3]
        sb_ = b("ssq%d" % (t % 2))
        act(junk, xf[r], AF.Square, [b("xf%d" % r)], [b("junk"), sb_], accum_out=s0)
        act(s1, s0, AF.Sqrt, [sb_, b("eps")], [sb_], accum=True, scale=1.0 / 2048.0, bias=epsc)
        recip(s2, s1, [sb_], [sb_], accum=True)
        stt(xf[r], xf[r], s2, gfin, ALU.mult, ALU.mult, [b("xf%d" % r), sb_, b("gfin")], [b("xf%d" % r)])
        dst = yo[t * 128:(t + 1) * 128, :] if t < 8 else ys
        dma("pool", dxf_st[r], dst, xf[r], [b("xf%d" % r)], [])
        if t + 3 < 9:
            fload(t + 3)

    p.emit()
    st.close()
    return nc


def _consts(hf):
    K = np.zeros((128, NK), np.float32)
    s = np.arange(128)
    K[:, 0:128] = np.eye(128, dtype=np.float32)
    K[:, 128:256] = (s[:, None] <= s[None, :]).astype(np.float32)
    K[:, 256:384] = ((s[:, None] // 8 == s[None, :] // 8) & (s[:, None] <= s[None, :])).astype(np.float32)
    K[:, 384:400] = (s[:, None] // 8 == np.arange(16)[None, :]).astype(np.float32)
    K[0:4, 400:404] = np.eye(4, dtype=np.float32)
    K[:, 404:420] = (s[:, None] == (8 * np.arange(16) + 7)[None, :]).astype(np.float32)
    for g, w in enumerate(POOL_W):
        for t in range(16):
            cnt = min(t + 1, w) if hf == 0 else w
            K[:, 420 + g * 16 + t] = 1.0 / cnt
    K[0:4, 484:612] = (s % 8 != 0).astype(np.float32)[None, :]
    K[0:4, 612:740] = np.where(s % 8 == 0, -1e30, 0.0).astype(np.float32)[None, :]
    K[0:4, 740:868] = 1.0
    return K


_NC = [None]


def kernel(x_prompt, x_sample, state_mlstm_C, state_mlstm_n, state_mlstm_m, state_pool_buf,
           c_prompt, c_sample, w_ada, b_ada, g_mix, w_in, b_gate, g_head, w_pool, pool_scale,
           w_out, g_ffn, w_ffn_in, w_ffn_out, g_final):
    f = lambda a: np.ascontiguousarray(np.asarray(a, dtype=np.float32))
    x_prompt = f(x_prompt); x_sample = f(x_sample)
    Cst_ = f(state_mlstm_C)[0]; nst = f(state_mlstm_n)[0]; mst = f(state_mlstm_m)[0]; bst = f(state_pool_buf)[0]
    c_prompt = f(c_prompt); c_sample = f(c_sample)
    shared = {"w_ada": f(w_ada)[0], "b_ada": f(b_ada)[0], "g_mix": f(g_mix)[0], "w_in": f(w_in)[0],
              "b_gate": f(b_gate)[0], "g_head": f(g_head)[0], "w_pool": f(w_pool)[0], "pool_scale": f(pool_scale)[0],
              "w_out": f(w_out)[0], "g_ffn": f(g_ffn)[0], "w_ffn_in": f(w_ffn_in)[0], "w_ffn_out": f(w_ffn_out)[0],
              "g_final": f(g_final)}
    in_maps = []
    for c in range(8):
        bb, hf = c // 2, c % 2
        sl = slice(16 * c, 16 * c + 16)
        m = dict(shared)
        m["xo"] = np.ascontiguousarray(x_prompt[bb, hf * 1024:(hf + 1) * 1024])
        m["xpre"] = np.ascontiguousarray(x_prompt[bb, 0:1024])
        m["xs"] = np.ascontiguousarray(x_sample[sl].reshape(128, 2048))
        m["cst"] = np.ascontiguousarray(np.concatenate([c_prompt[bb:bb + 1], c_sample[sl]], axis=0))
        m["Cs"] = np.ascontiguousarray(Cst_[sl].reshape(64, 256, 256))
        m["ns"] = np.ascontiguousarray(nst[sl].reshape(64, 256))
        m["ms"] = np.ascontiguousarray(mst[sl])
        m["bufs"] = np.ascontiguousarray(bst[sl])
        m["flag"] = np.full((128, 1), float(hf), np.float32)
        m["K"] = _consts(hf)
        in_maps.append(m)
    if _NC[0] is None:
        _NC[0] = build()
    res = run_bass_kernel_spmd(_NC[0], in_maps, core_ids=list(range(8)))
    R = res.results
    y_p = np.zeros((4, 2048, 2048), np.float32)
    for c in range(8):
        y_p[c // 2, (c % 2) * 1024:(c % 2 + 1) * 1024] = R[c]["yo"]
    y_s = np.concatenate([R[c]["ys"].reshape(16, 8, 2048) for c in range(8)], axis=0)
    C_p = np.stack([R[2 * i + 1]["Cp"] for i in range(4)])[None]
    n_p = np.stack([R[2 * i + 1]["npo"] for i in range(4)])[None]
    m_p = np.stack([R[2 * i + 1]["mpo"].reshape(4) for i in range(4)])[None]
    b_p = np.stack([R[2 * i + 1]["bufp"] for i in range(4)])[None]
    C_s = np.concatenate([R[c]["Cso"].reshape(16, 4, 256, 256) for c in range(8)], axis=0)[None]
    n_s = np.concatenate([R[c]["nso"].reshape(16, 4, 256) for c in range(8)], axis=0)[None]
    m_s = np.concatenate([R[c]["mso"] for c in range(8)], axis=0)[None]
    b_s = np.concatenate([R[c]["bufso"] for c in range(8)], axis=0)[None]
    return (y_p, y_s, C_p, n_p, m_p, b_p, C_s, n_s, m_s, b_s)
```
